# Optimizing a Trainium2 kernel written in Bass

```python
import jax, jax.numpy as jnp
from jax import lax
import numpy as np

D_MODEL = 2048
BATCH = 2
SEQ = 4096
DEPTH = 4
DEC_BATCH = 32
DEC_SEQ = 1
PAST_LEN = 16384
PAGE_SIZE = 128

N_A_LAYERS = DEPTH // 2
N_B_LAYERS = DEPTH - N_A_LAYERS
POOL_WINDOWS = (2, 4, 8, 16)
N_POOL_GROUPS = len(POOL_WINDOWS)
POOL_WIDTH = D_MODEL
POOL_GROUP = POOL_WIDTH // N_POOL_GROUPS
POOL_BUF = max(POOL_WINDOWS) - 1
HEAD_DIM = 64
N_HEADS = D_MODEL // HEAD_DIM
N_KV_HEADS = N_HEADS // 8
GQA_GROUP = N_HEADS // N_KV_HEADS
ATTN_WIDTH = N_HEADS * HEAD_DIM
WINDOW = 128
BLOCK = WINDOW
RMS_EPS = 1e-6
NEG_INF = -1e30

kernel_name = "yoco_pool_swa_sink_decoder_step"


def rms_norm(x, g):
    xf = x.astype(jnp.float32)
    y = xf * lax.rsqrt(jnp.mean(xf * xf, axis=-1, keepdims=True) + RMS_EPS)
    return (y * g.astype(jnp.float32)).astype(x.dtype)


def alibi_slopes():
    return jnp.exp2(-8.0 * jnp.arange(1, N_HEADS + 1, dtype=jnp.float32) / N_HEADS)


def multiscale_pool(u_ext, first_pos):
    B, L, E = u_ext.shape
    S = L - POOL_BUF
    uf = u_ext.astype(jnp.float32)
    cs = jnp.concatenate([jnp.zeros((B, 1, E), jnp.float32), jnp.cumsum(uf, axis=1)], axis=1)
    t_abs = first_pos + POOL_BUF + jnp.arange(S)
    end = cs[:, POOL_BUF + 1:]
    outs = []
    for g, w in enumerate(POOL_WINDOWS):
        sl = slice(g * POOL_GROUP, (g + 1) * POOL_GROUP)
        start = cs[:, POOL_BUF + 1 - w: POOL_BUF + 1 - w + S, sl]
        cnt = jnp.minimum(t_abs + 1, w).astype(jnp.float32)[None, :, None]
        outs.append((end[..., sl] - start) / cnt)
    mean = jnp.concatenate(outs, axis=-1)
    return (mean - uf[:, POOL_BUF:]).astype(u_ext.dtype)


def pool_layer(x, buf, first_pos, norm_g, w_in, w_grp, scale, w_out):
    h = rms_norm(x, norm_g)
    u, z = jnp.split(h @ w_in, 2, axis=-1)
    u_ext = jnp.concatenate([buf.astype(u.dtype), u], axis=1)
    p = multiscale_pool(u_ext, first_pos)
    B, S, _ = p.shape
    p = jnp.einsum('bsgc,gcd->bsgd', p.reshape(B, S, N_POOL_GROUPS, POOL_GROUP), w_grp)
    y = p.reshape(B, S, POOL_WIDTH) * scale * jax.nn.silu(z)
    return x + y @ w_out, u_ext[:, -POOL_BUF:]


def shared_kv(x, norm_g, w_kv, k_norm_g):
    h = rms_norm(x, norm_g)
    B, S, _ = h.shape
    kv = (h @ w_kv).reshape(B, S, 2, N_KV_HEADS, HEAD_DIM)
    k = rms_norm(kv[:, :, 0], k_norm_g)
    return k, kv[:, :, 1]


def sink_attention(q, k, v, q_pos, k_pos, sinks):
    qg = q.reshape(q.shape[:-2] + (N_KV_HEADS, GQA_GROUP, HEAD_DIM))
    s = jnp.einsum('...qkgd,...skd->...kgqs', qg, k,
                   preferred_element_type=jnp.float32) * (HEAD_DIM ** -0.5)
    dist = (q_pos[..., :, None] - k_pos[..., None, :])[..., None, None, :, :]
    slopes = alibi_slopes().reshape(N_KV_HEADS, GQA_GROUP, 1, 1)
    valid = (dist >= 0) & (dist < WINDOW) & (k_pos[..., None, None, None, :] >= 0)
    s = jnp.where(valid, s - slopes * dist.astype(jnp.float32), NEG_INF)
    sink = sinks.astype(jnp.float32).reshape(N_KV_HEADS, GQA_GROUP, 1, 1)
    m = jnp.maximum(jnp.max(s, axis=-1, keepdims=True), sink)
    p = jnp.exp(s - m)
    p = p / (jnp.sum(p, axis=-1, keepdims=True) + jnp.exp(sink - m))
    o = jnp.einsum('...kgqs,...skd->...qkgd', p.astype(v.dtype), v)
    return o.reshape(o.shape[:-3] + (ATTN_WIDTH,))


def band_blocks(t):
    B, S = t.shape[:2]
    tb = t.reshape(B, S // BLOCK, BLOCK, N_KV_HEADS, HEAD_DIM)
    prev = jnp.concatenate([jnp.zeros_like(tb[:, :1]), tb[:, :-1]], axis=1)
    return jnp.concatenate([prev, tb], axis=2)


def attn_layer(x, attend, norm_g, w_in, q_norm_g, w_out):
    h = rms_norm(x, norm_g)
    q, z = jnp.split(h @ w_in, 2, axis=-1)
    B, S, _ = q.shape
    q = rms_norm(q.reshape(B, S, N_HEADS, HEAD_DIM), q_norm_g)
    o = attend(q).reshape(B, S, ATTN_WIDTH)
    return x + (o * jax.nn.silu(z)) @ w_out


def setup_inputs(seed: int = 0) -> dict:
    key = jax.random.key(seed)
    ks = jax.random.split(key, 20)
    f32 = jnp.float32

    def nrm(k, shape, s):
        return jax.random.normal(k, shape, f32) * s

    win_buf = min(WINDOW, PAST_LEN)
    return {
        "x_prompt": nrm(ks[0], (BATCH, SEQ, D_MODEL), 1.0),
        "x_sample": nrm(ks[1], (DEC_BATCH, DEC_SEQ, D_MODEL), 1.0),
        "state_pool": nrm(ks[2], (N_A_LAYERS, DEC_BATCH, POOL_BUF, POOL_WIDTH), 1.0),
        "cache_k_win": nrm(ks[3], (DEC_BATCH, win_buf, N_KV_HEADS, HEAD_DIM), 1.0),
        "cache_v_win": nrm(ks[4], (DEC_BATCH, win_buf, N_KV_HEADS, HEAD_DIM), 1.0),
        "norm_a": 1.0 + nrm(ks[5], (N_A_LAYERS, D_MODEL), 0.1),
        "w_in_a": nrm(ks[6], (N_A_LAYERS, D_MODEL, 2 * POOL_WIDTH), D_MODEL ** -0.5),
        "w_grp_a": nrm(ks[7], (N_A_LAYERS, N_POOL_GROUPS, POOL_GROUP, POOL_GROUP), POOL_GROUP ** -0.5),
        "scale_a": 1.0 + nrm(ks[8], (N_A_LAYERS, POOL_WIDTH), 0.1),
        "w_out_a": nrm(ks[9], (N_A_LAYERS, POOL_WIDTH, D_MODEL), POOL_WIDTH ** -0.5),
        "norm_kv": 1.0 + nrm(ks[10], (D_MODEL,), 0.1),
        "w_kv": nrm(ks[11], (D_MODEL, 2 * N_KV_HEADS * HEAD_DIM), D_MODEL ** -0.5),
        "k_norm": 1.0 + nrm(ks[12], (HEAD_DIM,), 0.1),
        "norm_b": 1.0 + nrm(ks[13], (N_B_LAYERS, D_MODEL), 0.1),
        "w_in_b": nrm(ks[14], (N_B_LAYERS, D_MODEL, 2 * ATTN_WIDTH), D_MODEL ** -0.5),
        "q_norm": 1.0 + nrm(ks[15], (N_B_LAYERS, HEAD_DIM), 0.1),
        "sinks": nrm(ks[16], (N_B_LAYERS, N_HEADS), 1.0),
        "w_out_b": nrm(ks[17], (N_B_LAYERS, ATTN_WIDTH, D_MODEL), ATTN_WIDTH ** -0.5),
    }


def reference(x_prompt, x_sample, state_pool, cache_k_win, cache_v_win,
              norm_a, w_in_a, w_grp_a, scale_a, w_out_a,
              norm_kv, w_kv, k_norm,
              norm_b, w_in_b, q_norm, sinks, w_out_b):
    hp, hs = x_prompt, x_sample
    Bp, Sp = x_prompt.shape[:2]
    Ss = x_sample.shape[1]
    buf_len = cache_k_win.shape[1]
    pool_zero = jnp.zeros((Bp, POOL_BUF, POOL_WIDTH), x_prompt.dtype)
    pool_p, pool_s = [], []
    for l in range(DEPTH):
        if l < N_A_LAYERS:
            hp, bp = pool_layer(hp, pool_zero, -POOL_BUF,
                                norm_a[l], w_in_a[l], w_grp_a[l], scale_a[l], w_out_a[l])
            hs, bs = pool_layer(hs, state_pool[l], PAST_LEN - POOL_BUF,
                                norm_a[l], w_in_a[l], w_grp_a[l], scale_a[l], w_out_a[l])
            pool_p.append(bp)
            pool_s.append(bs)
            continue
        if l == N_A_LAYERS:
            kp, vp = shared_kv(hp, norm_kv, w_kv, k_norm)
            kn, vn = shared_kv(hs, norm_kv, w_kv, k_norm)
            kp_band, vp_band = band_blocks(kp), band_blocks(vp)
            blk0 = jnp.arange(Sp // BLOCK)[:, None] * BLOCK
            qpos_p = blk0 + jnp.arange(BLOCK)[None]
            kpos_p = blk0 - BLOCK + jnp.arange(2 * BLOCK)[None]
            ks_ext = jnp.concatenate([cache_k_win.astype(kn.dtype), kn], axis=1)
            vs_ext = jnp.concatenate([cache_v_win.astype(vn.dtype), vn], axis=1)
            qpos_s = PAST_LEN + jnp.arange(Ss)
            kpos_s = PAST_LEN - buf_len + jnp.arange(buf_len + Ss)
        j = l - N_A_LAYERS
        snk = sinks[j]
        attend_p = lambda q, snk=snk: sink_attention(
            q.reshape(Bp, Sp // BLOCK, BLOCK, N_HEADS, HEAD_DIM), kp_band, vp_band, qpos_p, kpos_p, snk)
        attend_s = lambda q, snk=snk: sink_attention(q, ks_ext, vs_ext, qpos_s, kpos_s, snk)
        hp = attn_layer(hp, attend_p, norm_b[j], w_in_b[j], q_norm[j], w_out_b[j])
        hs = attn_layer(hs, attend_s, norm_b[j], w_in_b[j], q_norm[j], w_out_b[j])
    pool_state_prompt = jnp.stack(pool_p, axis=0)
    pool_state_sample = jnp.stack(pool_s, axis=0)
    k_win_prompt = kp[:, -WINDOW:]
    v_win_prompt = vp[:, -WINDOW:]
    k_win_sample = ks_ext[:, -buf_len:]
    v_win_sample = vs_ext[:, -buf_len:]
    return (hp, hs, pool_state_prompt, pool_state_sample, k_win_prompt, v_win_prompt, k_win_sample, v_win_sample)
```

```python
import contextlib
import numpy as np
import concourse.bass as bass
import concourse.mybir as mybir
from concourse.bass_utils import run_bass_kernel_spmd

F32, BF16 = mybir.dt.float32, mybir.dt.bfloat16
AF = mybir.ActivationFunctionType
ALU = mybir.AluOpType
AX = mybir.AxisListType

D = 2048
NCH = 16
NS = 4
HALO = 160
MP = 512
NPASS = 2
C = NS + HALO + MP
T0 = (0, NS + HALO)
T1 = (NS + HALO, C)
TS = (0, NS)
NROWS = NS + HALO + MP * NPASS
EPS = 1e-6
BIG = 1.0e9
NCORES = 8


class Res:
    __slots__ = ("name", "w", "r")

    def __init__(self, name):
        self.name = name
        self.w = None
        self.r = {}


class Op:
    __slots__ = ("eng", "fn", "deps", "signal", "sigval", "dkey", "dval")

    def __init__(self, eng, fn, dkey):
        self.eng = eng
        self.fn = fn
        self.deps = []
        self.signal = False
        self.sigval = 0
        self.dkey = dkey
        self.dval = 0


class Sched:
    def __init__(self):
        self.ops = []
        self.dcnt = {}
        self.res = {}

    def R(self, name):
        r = self.res.get(name)
        if r is None:
            r = self.res[name] = Res(name)
        return r

    def add(self, eng, fn, reads=(), writes=(), dkey=None):
        op = Op(eng, fn, dkey)
        deps = set()
        for r in reads:
            r = self.R(r) if isinstance(r, str) else r
            if r.w is not None:
                deps.add(r.w)
        wl = []
        for w in writes:
            w = self.R(w) if isinstance(w, str) else w
            wl.append(w)
            if w.w is not None:
                deps.add(w.w)
            deps.update(w.r.values())
        for d in deps:
            if d is op:
                continue
            if d.dkey is None and d.eng == eng and eng == "pe":
                continue
            op.deps.append(d)
            if d.dkey is None:
                d.signal = True
        if dkey is not None:
            self.dcnt[dkey] = self.dcnt.get(dkey, 0) + 16
            op.dval = self.dcnt[dkey]
        rk = eng if dkey is None else ("dma", dkey)
        for r in reads:
            r = self.R(r) if isinstance(r, str) else r
            r.r[rk] = op
        for w in wl:
            w.w = op
            w.r = {}
        self.ops.append(op)
        return op


def build_program():
    nc = bass.Bass("TRN2", target_bir_lowering=False)
    S = Sched()
    es = contextlib.ExitStack()

    def din(name, shape):
        return nc.dram_tensor(name, list(shape), F32, kind="ExternalInput").ap()

    def dout(name, shape):
        return nc.dram_tensor(name, list(shape), F32, kind="ExternalOutput").ap()

    xc = din("xc", [NROWS, D])
    spool = din("spool", [2, NS * 15, D])
    ck = din("ck", [NS, 128, 256])
    cv = din("cv", [NS, 128, 256])
    w_in_a = din("w_in_a", [2, D, 2 * D])
    w_grp_a = din("w_grp_a", [2, 4, 512, 512])
    w_out_a = din("w_out_a", [2, D, D])
    w_kv = din("w_kv", [D, 512])
    w_in_b = din("w_in_b", [2, D, 2 * D])
    w_out_b = din("w_out_b", [2, D, D])
    gains_d = din("gains", [128, 7 * NCH])
    sinks_d = din("sinks_l", [128, 2 * NCH])
    gkd_d = din("gk_dup", [128, 1])
    gqd_d = din("gq_dup", [128, 2])
    qk_d = din("qk_rows", [3, 64])
    ident_d = din("ident", [128, 128])
    dist_d = din("dist", [128, 3 * 128])
    sbias_d = din("sbias", [128, 2 * NCH])
    invc_d = din("invc", [128, 4 * 16])

    y_main = dout("y_main", [MP * NPASS, D])
    y_samp = dout("y_samp", [NS, D])
    pool_p = dout("pool_p", [2, 15, D])
    pool_s = dout("pool_s", [2, NS, 15, D])
    kwin_p = dout("kwin_p", [128, 256])
    vwin_p = dout("vwin_p", [128, 256])
    kwin_s = dout("kwin_s", [NS, 128, 256])
    vwin_s = dout("vwin_s", [NS, 128, 256])

    def sb(name, shape, dt):
        return es.enter_context(nc.sbuf_tensor(name, list(shape), dt))

    xres = sb("xres", [128, NCH, C], F32)
    h = sb("h", [128, NCH, C], BF16)
    y = sb("y", [128, NCH, C], BF16)
    wsl = [sb(f"wsl{i}", [128, 4096], BF16) for i in range(4)]
    kTp = sb("kTp", [128, 2, 4, 640], BF16)
    vtm = sb("vtm", [128, 5, 576], BF16)
    ksd = sb("ksd", [128, NS, 4, 128], BF16)
    vs = sb("vs", [128, NS, 576], BF16)
    scrf = sb("scrf", [128, 6696], F32)
    scrb = sb("scrb", [128, 8224], BF16)
    spoolT = sb("spoolT", [128, NCH, NS, 16], F32)
    unew = sb("unew", [128, 2, NCH, NS], F32)
    ustate = sb("ustate", [128, 2, NCH, 16], F32)
    rstd = sb("rstd", [128, C], F32)
    srt = rstd
    sqb = sb("sqb", [128, C], BF16)
    sqbB = sb("sqbB", [128, NS + MP], BF16)
    sqb2 = [sqb, sqbB]
    ident = sb("ident_s", [128, 128], F32)
    onesD = sb("onesD", [128, 128], BF16)
    ones128 = sb("ones128", [128, 128], BF16)
    blk64 = sb("blk64", [128, 128], BF16)
    onespad = sb("onespad", [128, 192], BF16)
    dist = sb("dist_s", [128, 3, 128], F32)
    sbias = sb("sbias_s", [128, 2, NCH], F32)
    invc = sb("invc_s", [128, 4, 16], F32)
    gains = sb("gains_s", [128, 7, NCH], F32)
    sinks = sb("sinks_s", [128, 2, NCH], F32)
    esink = sb("esink", [128, 2, NCH], F32)
    gkd = sb("gkd", [128, 1], F32)
    gq8 = sb("gq8", [128, 2], F32)
    qkb = sb("qkb", [128, 3, 64], F32)
    prod = sb("prod", [128, 64], F32)
    negM = sb("negM", [128, 2], F32)
    qspad = sb("qspad", [128, 2, 2, 4, NS], BF16)
    kout = sb("kout", [128, 256], F32)
    vout = sb("vout", [128, 256], F32)
    ktm = sb("ktm", [128, 256], F32)
    kss = sb("kss", [128, 8], F32)
    smallf = sb("smallf", [128, 64], F32)
    pTs = sb("pTs", [128, 32], BF16)
    cstage = sb("cstage", [128, NS, 256], BF16)
    vstage = sb("vstage", [128, NS, 256], BF16)
    ident_b = sb("ident_b", [128, 128], BF16)
    ps = [es.enter_context(nc.psum_tensor(f"ps{i}", [128, 512], F32)) for i in range(8)]
    _yflat = y[:].rearrange("p c n -> p (c n)")
    stgs = [_yflat[:, 0:4096].bitcast(F32), _yflat[:, 4096:8192].bitcast(F32)]
    STGR = ["stgA", "stgB", "ydup"]
    ydup = _yflat[:, 8192:8192 + NS * 512].rearrange("p (j g d e) -> p j g d e", j=NS, g=4, d=2)
    YALL = [f"y{c}_{t0}" for c in range(NCH) for t0 in (0, T1[0])]

    ubuf = [scrf[:, 0:688], scrf[:, 688:1376]]
    tmpb = [scrf[:, 1376:2064], scrf[:, 2064:2752]]
    szA = [scrf[:, 2752:3428], scrf[:, 3428:4104]]
    szB = [scrf[:, 0:2064].rearrange("p (m n) -> p m n", m=4), scrf[:, 2064:4128].rearrange("p (m n) -> p m n", m=4)]
    sbf = [scrf[:, 4128:4640], scrf[:, 4640:5152]]
    t1 = scrf[:, 5152:5664]
    qraw = [scrf[:, 5664:6180], scrf[:, 6180:6696]]
    kraw = scrf[:, 4128:4128 + C]
    krs = scrf[:, 4804:4804 + C]
    pbuf = [scrb[:, 0:2704].rearrange("p (m n) -> p m n", m=4),
            scrb[:, 2704:5408].rearrange("p (m n) -> p m n", m=4)]
    qn = [scrb[:, 0:2064].rearrange("p (m n) -> p m n", m=4), scrb[:, 2064:4128].rearrange("p (m n) -> p m n", m=4)]
    pT = [scrb[:, 4128:6176].rearrange("p (a b n) -> p a b n", a=2, b=2),
          scrb[:, 6176:8224].rearrange("p (a b n) -> p a b n", a=2, b=2)]
    SCR = []
    ksq = scrf[:, 5480:5736]
    knew = kout[0:NS, :]
    vnew = vout[0:NS, :]

    def vg(arr_ap):
        return arr_ap[:, 64:576].rearrange("p (g e) -> p g e", e=128)[:, :, 0:64]

    def wview(i, k, n):
        return wsl[i][:, 0:k * n].rearrange("p (k n) -> p k n", k=k)

    wfifo = [0, 1, 2, 3]

    def unpin(i):
        wfifo.append(i)

    def load_panel(src_aps, dst_fn, pinned=False):
        i = wfifo.pop(0)
        if not pinned:
            wfifo.append(i)
        for dv, sa in zip(dst_fn, src_aps):
            def fn(e, dv=dv, sa=sa, i=i):
                return e.dma_start(out=dv(i), in_=sa)
            S.add("pool", fn, reads=["xloaded"], writes=[f"wsl{i}"], dkey=f"wsl{i}")
        return i

    def load_std(wap, col0, ncol=256):
        src = wap.rearrange("(k p) n -> p k n", p=128)[:, :, col0:col0 + ncol]
        return load_panel([src], [lambda i: wview(i, 16, ncol)])

    def dma_sp(out, in_, reads=(), writes=(), dkey="par"):
        def fn(e):
            return e.dma_start(out=out, in_=in_)
        return S.add("sp", fn, reads=reads, writes=writes, dkey=dkey)

    P = "params"
    dma_sp(gains[:].rearrange("p a c -> p (a c)"), gains_d, writes=[P])
    dma_sp(sinks[:].rearrange("p a c -> p (a c)"), sinks_d, writes=[P])
    dma_sp(gkd[:], gkd_d, writes=[P])
    dma_sp(gq8[:], gqd_d, writes=[P])
    dma_sp(qkb[:].rearrange("p a c -> p (a c)"), qk_d.rearrange("a c -> (a c)").partition_broadcast(128), writes=[P])
    dma_sp(ident[:], ident_d, writes=[P])
    dma_sp(dist[:].rearrange("p a c -> p (a c)"), dist_d, writes=[P])
    dma_sp(sbias[:].rearrange("p a c -> p (a c)"), sbias_d, writes=[P])
    dma_sp(invc[:].rearrange("p a c -> p (a c)"), invc_d, writes=[P])

    def vec(fn, reads=(), writes=()):
        return S.add("dve", fn, reads=reads, writes=writes)

    def act(fn, reads=(), writes=()):
        return S.add("act", fn, reads=reads, writes=writes)

    def pe(fn, reads=(), writes=()):
        return S.add("pe", fn, reads=reads, writes=writes)

    def init_consts(e):
        e.memset(onesD[:], 1.0 / D)
        e.memset(ones128[:], 1.0 / 128)
        e.memset(blk64[:], 0.0)
        return e.memset(onespad[:], 0.0)

    def init_consts2(e):
        e.memset(blk64[0:64, 0:64], 1.0 / 64)
        e.memset(blk64[64:128, 64:128], 1.0 / 64)
        return e.memset(onespad[:, 64:128], 1.0)

    def init_zeros(e):
        ins = None
        for t_ in (scrf, scrb, kTp, vtm, vs, ksd, qspad, ustate, spoolT):
            ins = e.memzero(t_[:])
        return ins

    def init_cst(e):
        e.memzero(cstage[:])
        return e.memzero(vstage[:])
    act(init_cst, writes=[f"cst{j}" for j in range(NS)] + [f"vst{j}" for j in range(NS)])
    vec(init_consts, writes=["consts"])
    vec(lambda e: e.tensor_copy(out=ident_b[:], in_=ident[:]), reads=[P], writes=["ident_b"])
    vec(init_consts2, reads=["consts"], writes=["consts"])
    act(init_zeros, writes=["kTp", "vtm", "vs", "ksd", "qspad", "ubuf0", "ubuf1", "tmp0", "tmp1", "szA0", "szA1",
                            "pbuf0", "pbuf1", "ustate", "spoolT", "kraw", "qspad0", "qspad1"])

    for j in range(2):
        for fn in (lambda e, j=j: e.tensor_tensor(out=prod[:], in0=qkb[:, j, :], in1=qkb[:, 2, :], op=ALU.mult),
                   lambda e, j=j: e.tensor_reduce(out=negM[:, j:j + 1], in_=prod[:], axis=AX.X, op=ALU.max, apply_absolute_value=True),
                   lambda e, j=j: e.tensor_scalar(out=negM[:, j:j + 1], in0=negM[:, j:j + 1], scalar1=-8.0, scalar2=None, op0=ALU.mult)):
            vec(fn, reads=[P, "negM"], writes=["negM"])

        def fn2(e, j=j):
            return e.activation(out=esink[:, j, :], in_=sinks[:, j, :], func=AF.Exp, bias=negM[:, j:j + 1], scale=1.0)
        act(fn2, reads=[P, "negM"], writes=["esink"])
    vec(lambda e: e.tensor_scalar(out=gq8[:], in0=gq8[:], scalar1=0.125, scalar2=None, op0=ALU.mult),
        reads=[P], writes=["gq8"])

    bank_rr = {"i": 0}

    def next_pair():
        i = bank_rr["i"] % 3
        bank_rr["i"] += 1
        return 2 * i, 2 * i + 1

    def hres(c, t):
        return f"h{c}_{t[0]}"

    def xr(c, t):
        return f"x{c}_{t[0]}"

    def yres(c, t):
        return f"y{c}_{t[0]}"

    def pv(banks, t, a=None, b=None):
        bk, c0 = banks[t]
        n = t[1] - t[0]
        a = 0 if a is None else a
        b = n if b is None else b
        return ps[bk][:, c0 + a:c0 + b]

    def pr(banks, tiles):
        return [f"ps{banks[t][0]}" for t in tiles]

    def chunk_matmul(slot, wk, wcol, rhs_arr, rhs_res_fn, tiles, nk=NCH, kview=None, extra_reads=(), dest=None, ksplit=None):
        if dest is None:
            b0, b1 = next_pair()
            banks = {}
            for t in tiles:
                banks[t] = (b0 if (t[1] - t[0]) < 512 else b1, 0)
            if len(tiles) == 2 and banks[tiles[0]][0] == banks[tiles[1]][0]:
                banks[tiles[1]] = (b1, 0)
        else:
            banks = dest
        wv = kview if kview is not None else wview(slot, wk, 256)

        ks = ksplit if ksplit else nk
        for k0 in range(0, nk, ks):
            def fn(e, k0=k0):
                ins = None
                for k in range(k0, min(nk, k0 + ks)):
                    for t in tiles:
                        ins = e.matmul(pv(banks, t), lhsT=wv[:, k, wcol:wcol + 128],
                                       rhs=rhs_arr[:, k, t[0]:t[1]], start=(k == 0), stop=(k == nk - 1))
                return ins
            reads = [f"wsl{slot}"] + [rhs_res_fn(k, t) for k in range(k0, min(nk, k0 + ks)) for t in tiles] + list(extra_reads)
            pe(fn, reads=reads, writes=pr(banks, tiles))
        return banks

    def stat_dest(t):
        return (6, 0) if (t[1] - t[0]) == 512 else (7, 0)

    def emit_square(c, t):
        act(lambda e: e.activation(out=h[:, c, t[0]:t[1]], in_=xres[:, c, t[0]:t[1]], func=AF.Square),
            reads=[xr(c, t)], writes=[hres(c, t)])

    def emit_stat(c, t):
        bk, c0 = stat_dest(t)
        n = t[1] - t[0]
        pe(lambda e: e.matmul(ps[bk][:, c0:c0 + n], lhsT=onesD[:], rhs=h[:, c, t[0]:t[1]], start=(c == 0), stop=(c == NCH - 1)),
           reads=["consts", hres(c, t)], writes=[f"ps{bk}"])

    def rmsnorm(gi, tiles, mode="full"):
        if mode == "full":
            for t in tiles:
                for c in range(NCH):
                    emit_square(c, t)
                for c in range(NCH):
                    emit_stat(c, t)
        for t in tiles:
            n = t[1] - t[0]
            if mode != "reuse":
                bk, c0 = stat_dest(t)
                chain("act", [lambda e, t=t, n=n, bk=bk, c0=c0: e.activation(out=rstd[:, t[0]:t[1]], in_=ps[bk][:, c0:c0 + n], func=AF.Ln,
                                                                             bias=EPS, scale=1.0),
                              lambda e, t=t: e.activation(out=rstd[:, t[0]:t[1]], in_=rstd[:, t[0]:t[1]], func=AF.Exp, scale=-0.5)],
                      reads=[f"ps{bk}"], writes=["rstd"])
        for c in range(NCH):
            for t in tiles:
                def fh(e, c=c, t=t):
                    return e.scalar_tensor_tensor(out=h[:, c, t[0]:t[1]], in0=xres[:, c, t[0]:t[1]],
                                                  scalar=gains[:, gi, c:c + 1], in1=rstd[:, t[0]:t[1]],
                                                  op0=ALU.mult, op1=ALU.mult)
                vec(fh, reads=[xr(c, t), "rstd", P], writes=[hres(c, t)])

    def residual_add(banks, c, tiles):
        for t in tiles:
            n = t[1] - t[0]

            def fn(e, t=t, n=n):
                return e.tensor_tensor(out=xres[:, c, t[0]:t[1]], in0=pv(banks, t), in1=xres[:, c, t[0]:t[1]], op=ALU.add)
            vec(fn, reads=pr(banks, [t]) + [xr(c, t)], writes=[xr(c, t)])

    def w_out_phase(wap, tiles, norm_tiles=None, hook=None):
        pending = []
        for jj in range(8):
            slot = load_std(wap, jj * 256)
            if jj == 2 and hook is not None:
                hook()
            for mm in range(2):
                c = 2 * jj + mm
                banks = chunk_matmul(slot, 16, mm * 128, y, yres, tiles)
                residual_add(banks, c, tiles)
                if norm_tiles:
                    for t in norm_tiles:
                        emit_square(c, t)
                    pending.append(c)
                    if len(pending) > 2:
                        c2 = pending.pop(0)
                        for t in norm_tiles:
                            emit_stat(c2, t)
        for c2 in pending:
            for t in norm_tiles:
                emit_stat(c2, t)

    def xall(tiles):
        return [xr(c, t) for c in range(NCH) for t in tiles]

    def hall(tiles):
        return [hres(c, t) for c in range(NCH) for t in tiles]

    def yall(tiles):
        return [yres(c, t) for c in range(NCH) for t in tiles]

    ALLT = [T0, T1, TS]

    def load_x(p):
        if p == 0:
            row_tiles = [(r, min(r + 128, C)) for r in range(0, C, 128)]
            coff = 0
        else:
            row_tiles = [(C + r, C + r + 128) for r in range(0, MP, 128)]
            coff = T1[0] - C
        for ti, (r0, r1) in enumerate(row_tiles):
            nr = r1 - r0
            st = stgs[ti % 2]
            sr = STGR[ti % 2]
            dma_sp(st[0:nr, :], xc[r0:r1, :], writes=[sr] + (YALL if ti == 0 else []) + (["xloaded"] if (p == 0 and ti == 1) else []),
                   dkey=sr)
            for cq in range(4):
                bk = 6 + (cq % 2)

                def fp(e, cq=cq, bk=bk, st=st):
                    ins = None
                    for i in range(4):
                        c = 4 * cq + i
                        ins = e.transpose(ps[bk][:, i * 128:(i + 1) * 128], st[:, c * 128:(c + 1) * 128], ident[:])
                    return ins
                pe(fp, reads=[sr, P], writes=[f"ps{bk}"])
                c0 = r0 + coff

                def fv(e, cq=cq, bk=bk, nr=nr, c0=c0):
                    return e.tensor_copy(out=xres[:, 4 * cq:4 * cq + 4, c0:c0 + nr],
                                         in_=ps[bk][:].rearrange("p (c n) -> p c n", c=4)[:, :, 0:nr])
                vec(fv, reads=[f"ps{bk}"], writes=[xr(c, t) for c in range(4 * cq, 4 * cq + 4) for t in ALLT])

    spstg = scrf[:, 0:2048]
    SPG = ["ubuf0", "ubuf1", "tmp0"]

    def load_spool(l):
        dma_sp(spstg[0:NS * 15, :], spool[l], writes=["spstg"] + SPG, dkey="spstg")
        for cq in range(4):
            _, bk = next_pair()

            def fp(e, cq=cq, bk=bk):
                ins = None
                for i in range(4):
                    c = 4 * cq + i
                    ins = e.transpose(ps[bk][:, i * 128:(i + 1) * 128], spstg[:, c * 128:(c + 1) * 128], ident[:])
                return ins
            pe(fp, reads=["spstg", P], writes=[f"ps{bk}"])

            def fv(e, cq=cq, bk=bk):
                return e.tensor_copy(out=spoolT[:, 4 * cq:4 * cq + 4, :, 0:15],
                                     in_=ps[bk][:].rearrange("p (c n) -> p c n", c=4)[:, :, 0:60].rearrange("p c (j r) -> p c j r", j=NS))
            vec(fv, reads=[f"ps{bk}"], writes=["spoolT"])

        def fzero(e):
            e.memset(ubuf[0][:, 0:16], 0.0)
            return e.memset(ubuf[1][:, 0:16], 0.0)
        vec(fzero, reads=["spstg"], writes=["spstg"] + SPG)

    tokc = {"i": 0}

    def chain(eng, fns, reads=(), writes=()):
        tokc["i"] += 1
        tok = f"_tok{tokc['i']}"
        op = None
        for i, fn in enumerate(fns):
            rd = list(reads) + ([tok] if i > 0 else [])
            op = S.add(eng, fn, reads=rd, writes=list(writes) + [tok])
        return op

    def a_group(p, l, g, tiles):
        wa = w_in_a[l]
        w = 2 << g
        pb = pbuf[g % 2]
        pres = f"pbuf{g % 2}"
        for hp in range(2):
            slot = load_std(wa, g * 512 + hp * 256)
            for mm in range(2):
                m = 2 * hp + mm
                c = 4 * g + m
                banks = chunk_matmul(slot, 16, mm * 128, h, hres, tiles, ksplit=(2 if (g == 0 and hp == 0) else None))
                ub = ubuf[c % 2]
                ubr = f"ubuf{c % 2}"
                if p == 1:
                    vec(lambda e, ub=ub, c=c: e.tensor_copy(out=ub[:, 160:176], in_=ustate[:, l, c, :]),
                        reads=["ustate"], writes=[ubr])

                def fe(e, ub=ub, banks=banks, c=c):
                    ins = e.activation(out=ub[:, 176:688], in_=pv(banks, T1), func=AF.Copy)
                    if p == 0:
                        ins = e.activation(out=ub[:, 16:176], in_=pv(banks, T0, NS, NS + HALO), func=AF.Copy)
                        ins = e.activation(out=spoolT[:, c, :, 15], in_=pv(banks, T0, 0, NS), func=AF.Copy)
                        ins = e.activation(out=unew[:, l, c, :], in_=pv(banks, T0, 0, NS), func=AF.Copy)
                    return ins
                act(fe, reads=pr(banks, tiles), writes=[ubr, "spoolT", "unew"])
                vec(lambda e, ub=ub, c=c: e.tensor_copy(out=ustate[:, l, c, :], in_=ub[:, 672:688]),
                    reads=[ubr], writes=["ustate"])
                cur = ub
                curr = ubr
                step = 1
                ti = 0
                while step < w:
                    dst = tmpb[ti % 2]
                    dstr = f"tmp{ti % 2}"
                    lo = 2 * step - 1 + (160 if p == 1 else 0)

                    def fs(e, cur=cur, dst=dst, lo=lo, step=step):
                        return e.tensor_tensor(out=dst[:, lo:688], in0=cur[:, lo:688], in1=cur[:, lo - step:688 - step], op=ALU.add)
                    vec(fs, reads=[curr], writes=[dstr])
                    cur, curr = dst, dstr
                    step *= 2
                    ti += 1
                c0 = NS if p == 0 else T1[0]
                u0 = c0 + 12
                fns = [lambda e, cur=cur, ub=ub, m=m: e.scalar_tensor_tensor(
                    out=pb[:, m, c0:C], in0=cur[:, u0:688], scalar=1.0 / w, in1=ub[:, u0:688], op0=ALU.mult, op1=ALU.subtract)]
                if p == 0:
                    fns.append(lambda e, cur=cur: e.tensor_tensor(out=smallf[:, 0:16], in0=cur[:, 176:192], in1=invc[:, g, :], op=ALU.mult))
                    fns.append(lambda e, ub=ub, m=m: e.tensor_tensor(out=pb[:, m, T1[0]:T1[0] + 16], in0=smallf[:, 0:16],
                                                                      in1=ub[:, 176:192], op=ALU.subtract))
                    fns.append(lambda e, c=c: e.tensor_reduce(out=smallf[:, 16:16 + NS], in_=spoolT[:, c, :, 16 - w:16], axis=AX.X, op=ALU.add))
                    fns.append(lambda e, c=c, m=m: e.scalar_tensor_tensor(out=pb[:, m, 0:NS], in0=smallf[:, 16:16 + NS], scalar=1.0 / w,
                                                                         in1=spoolT[:, c, :, 15], op0=ALU.mult, op1=ALU.subtract))
                chain("dve", fns, reads=[curr, ubr, P, "spoolT"], writes=[pres, "smallf"])
        gsl = load_panel([w_grp_a[l, g].rearrange("(k p) n -> p k n", p=128)], [lambda i: wview(i, 4, 512)], pinned=True)
        gview = wview(gsl, 4, 512)
        for hp in range(2):
            slot = load_std(wa, D + g * 512 + hp * 256)
            zbs = []
            for mm in range(2):
                m = 2 * hp + mm
                c = 4 * g + m
                zb = chunk_matmul(slot, 16, mm * 128, h, hres, tiles)
                sz = szA[c % 2]
                szr = f"szA{c % 2}"

                def fz(e, zb=zb, sz=sz):
                    ins = None
                    for t in tiles:
                        ins = e.activation(out=sz[:, t[0]:t[1]], in_=pv(zb, t), func=AF.Silu)
                    return ins
                act(fz, reads=pr(zb, tiles), writes=[szr])
            for mm in range(2):
                m = 2 * hp + mm
                c = 4 * g + m
                sz = szA[c % 2]
                szr = f"szA{c % 2}"
                gb = chunk_matmul(gsl, 4, m * 128, pb, lambda k, t: pres, tiles, nk=4, kview=gview)

                def fy(e, gb=gb, sz=sz, c=c):
                    ins = None
                    for t in tiles:
                        ins = e.scalar_tensor_tensor(out=y[:, c, t[0]:t[1]], in0=pv(gb, t),
                                                     scalar=gains[:, 5 + l, c:c + 1], in1=sz[:, t[0]:t[1]],
                                                     op0=ALU.mult, op1=ALU.mult)
                    return ins
                vec(fy, reads=pr(gb, tiles) + [szr, P], writes=[yres(c, t) for t in tiles])
        unpin(gsl)

    def a_layer(p, l):
        tiles = [T0, T1] if p == 0 else [T1]
        rmsnorm(l, tiles, mode=("full" if l == 0 else "stats_ready"))
        if p == 0:
            for j in range(NS):
                dma_sp(pool_s[l, j, 0:14, :], spool[l, j * 15 + 1:j * 15 + 15, :], dkey="misc")
        for g in range(4):
            a_group(p, l, g, tiles)
        w_out_phase(w_out_a[l], tiles, norm_tiles=tiles,
                    hook=((lambda: load_spool(1)) if (p == 0 and l == 0) else None))

    def kv_kfm(p, g, slot, gi, tiles):
        kb = chunk_matmul(slot, 16, gi * 128, h, hres, tiles, ksplit=(2 if g == 0 else None))

        def fk(e):
            ins = None
            for t in tiles:
                n = t[1] - t[0]
                e.activation(out=kraw[:, t[0]:t[1]], in_=pv(kb, t), func=AF.Copy)
                ins = e.activation(out=sqb[:, t[0]:t[1]], in_=pv(kb, t), func=AF.Square)
            return ins
        act(fk, reads=pr(kb, tiles), writes=["kraw", "sqb"])
        for t in tiles:
            n = t[1] - t[0]
            pe(lambda e, t=t, n=n: e.matmul(ps[6][:, 0:n], lhsT=ones128[:], rhs=sqb[:, t[0]:t[1]], start=True, stop=True),
               reads=["sqb", "consts"], writes=["ps6"])
            chain("act", [lambda e, t=t, n=n: e.activation(out=krs[:, t[0]:t[1]], in_=ps[6][:, 0:n], func=AF.Ln, bias=EPS, scale=1.0),
                          lambda e, t=t: e.activation(out=krs[:, t[0]:t[1]], in_=krs[:, t[0]:t[1]], func=AF.Exp, scale=-0.5)],
                  reads=["ps6"], writes=["krs"])
        lo = tiles[0][0]
        fns = [lambda e: e.scalar_tensor_tensor(out=kraw[:, lo:C], in0=kraw[:, lo:C], scalar=gkd[:, 0:1], in1=krs[:, lo:C],
                                                op0=ALU.mult, op1=ALU.mult)]

        def fcp(e):
            e.tensor_copy(out=kTp[0:64, 0, g, 128:640], in_=kraw[0:64, T1[0]:T1[1]])
            ins = e.tensor_copy(out=kTp[64:128, 1, g, 128:640], in_=kraw[64:128, T1[0]:T1[1]])
            if p == 0:
                e.tensor_copy(out=kTp[0:64, 0, g, 0:128], in_=kraw[0:64, T0[1] - 128:T0[1]])
                e.tensor_copy(out=kTp[64:128, 1, g, 0:128], in_=kraw[64:128, T0[1] - 128:T0[1]])
                ins = e.tensor_copy(out=ksd[:, :, g, 127], in_=kraw[:, 0:NS])
            return ins
        fns.append(fcp)
        chain("dve", fns, reads=["krs", "kraw", P], writes=["kraw", "kTp", "ksd"])

    def tm_k(rows, c0, dst, dres, slk, kv_k, tiles):
        def fk(e):
            ins = None
            for k in range(NCH):
                ins = e.matmul(ps[7][0:rows, 256:512], lhsT=h[:, k, c0:c0 + rows], rhs=kv_k[:, k, :], start=(k == 0), stop=(k == NCH - 1))
            return ins
        pe(fk, reads=[f"wsl{slk}"] + hall(tiles), writes=["ps7"])
        g4 = lambda ap: ap.rearrange("p (g e) -> p g e", g=4)
        chain("dve", [
            lambda e: e.tensor_copy(out=ktm[0:rows, :], in_=ps[7][0:rows, 256:512]),
            lambda e: e.tensor_tensor(out=ksq[0:rows, :], in0=ktm[0:rows, :], in1=ktm[0:rows, :], op=ALU.mult),
            lambda e: e.tensor_reduce(out=kss[0:rows, 0:4], in_=g4(ksq[0:rows, :]), axis=AX.X, op=ALU.add),
        ], reads=["ps7"], writes=["ktm", "kss", "ksq"])
        act(lambda e: e.activation(out=kss[0:rows, 4:8], in_=kss[0:rows, 0:4], func=AF.Sqrt, bias=EPS, scale=1.0 / 64),
            reads=["kss"], writes=["kss2"])
        chain("dve", [
            lambda e: e.reciprocal(out=kss[0:rows, 0:4], in_=kss[0:rows, 4:8]),
            lambda e: e.tensor_tensor(out=g4(ktm[0:rows, :]), in0=g4(ktm[0:rows, :]),
                                      in1=kss[0:rows, 0:4].unsqueeze(2).broadcast_to([rows, 4, 64]), op=ALU.mult),
            lambda e: e.tensor_tensor(out=g4(dst), in0=g4(ktm[0:rows, :]),
                                      in1=qkb[0:rows, 2, :].unsqueeze(1).broadcast_to([rows, 4, 64]), op=ALU.mult),
        ], reads=["kss2", "ktm", P], writes=["kss", "ktm", dres])

    def kv_phase(p):
        tiles = [T0, T1] if p == 0 else [T1]
        rmsnorm(2, tiles, mode="stats_ready")
        wkv3 = w_kv.rearrange("(k p) n -> p k n", p=128)
        if p == 1:
            chain("dve", [lambda e: e.tensor_copy(out=kTp[:, :, :, 0:128], in_=kTp[:, :, :, 512:640]),
                          lambda e: e.tensor_copy(out=vtm[:, 0, :], in_=vtm[:, 4, :])],
                  reads=["kTp", "vtm"], writes=["kTp", "vtm"])
        for gh in range(2):
            srcs, dsts = [], []
            for d in range(2):
                for gi in range(2):
                    srcs.append(wkv3[:, :, gh * 128 + gi * 64:gh * 128 + gi * 64 + 64])
                    dsts.append(lambda i, d=d, gi=gi: wsl[i][:].rearrange("p (k g d e) -> p k g d e", k=16, g=2, d=2)[:, :, gi, d, :])
            slot = load_panel(srcs, dsts)
            for gi in range(2):
                kv_kfm(p, 2 * gh + gi, slot, gi, tiles)
        if p == 0:
            transpose_caches()
        slk = load_std(w_kv, 0)
        slv = load_std(w_kv, 256)
        kv_k = wview(slk, 16, 256)
        kv_v = wview(slv, 16, 256)
        blocks = []
        if p == 0:
            blocks.append((T0[1] - 128, 128, 0))
        for bi in range(4):
            blocks.append((T1[0] + bi * 128, 128, bi + 1))
        for vi, (c0, ncol, blk) in enumerate(blocks):
            vb = 7 if vi % 2 == 0 else 6

            def fv(e, c0=c0, ncol=ncol, vb=vb):
                ins = None
                for k in range(NCH):
                    ins = e.matmul(ps[vb][0:ncol, 0:256], lhsT=h[:, k, c0:c0 + ncol], rhs=kv_v[:, k, :], start=(k == 0), stop=(k == NCH - 1))
                return ins
            pe(fv, reads=[f"wsl{slv}"] + hall(tiles), writes=[f"ps{vb}"])
            last = (p == 1 and blk == 4)

            def fe(e, blk=blk, last=last, vb=vb):
                ins = e.tensor_copy(out=vg(vtm[:, blk, :]), in_=ps[vb][:, 0:256].rearrange("p (g e) -> p g e", g=4))
                if last:
                    ins = e.tensor_copy(out=vout[:], in_=ps[vb][:, 0:256])
                return ins
            vec(fe, reads=[f"ps{vb}"], writes=["vtm", "vout"])
            if last:
                outs.append(dma_sp(vwin_p, vout[:], reads=["vout"], dkey="vout"))
        if p == 1:
            tm_k(128, T1[1] - 128, kout[:], "kout", slk, kv_k, tiles)
            outs.append(dma_sp(kwin_p, kout[:], reads=["kout"], dkey="kout"))
        else:
            tm_k(NS, 0, knew, "kout", slk, kv_k, tiles)

            def fvs(e):
                ins = None
                for k in range(NCH):
                    ins = e.matmul(ps[7][0:NS, 0:256], lhsT=h[:, k, 0:NS], rhs=kv_v[:, k, :], start=(k == 0), stop=(k == NCH - 1))
                return ins
            pe(fvs, reads=[f"wsl{slv}"] + hall(tiles), writes=["ps7"])
            vec(lambda e: e.tensor_copy(out=vnew, in_=ps[7][0:NS, 0:256]), reads=["ps7"], writes=["vout"])
            for j in range(NS):
                outs.append(dma_sp(kwin_s[j, 127:128, :], knew[j:j + 1, :], reads=["kout"], dkey="kout"))
                outs.append(dma_sp(vwin_s[j, 127:128, :], vnew[j:j + 1, :], reads=["vout"], dkey="vout"))

                def fd(e, j=j):
                    return e.dma_start(out=vg(vs[127:128, j, :]), in_=vnew[j:j + 1, :].rearrange("p (g e) -> p g e", g=4))
                S.add("pool", fd, reads=["vout", "vs"], writes=["vs"], dkey="vs")

    def load_caches():
        for j in range(NS):
            S.add("pool", lambda e, j=j: e.dma_start(out=cstage[0:127, j, :], in_=ck[j, 1:128, :]),
                  writes=[f"cst{j}"], dkey=f"cst{j}")
            S.add("pool", lambda e, j=j: e.dma_start(out=vstage[0:127, j, :], in_=cv[j, 1:128, :]),
                  writes=[f"vst{j}"], dkey=f"vst{j}")
            dma_sp(kwin_s[j, 0:127, :], ck[j, 1:128, :], dkey="misc")
            dma_sp(vwin_s[j, 0:127, :], cv[j, 1:128, :], dkey="misc")

    def transpose_caches():
        psb = ps[7][:].bitcast(BF16)
        for j in range(NS):
            vec(lambda e, j=j: e.tensor_copy(out=vg(vs[:, j, :]), in_=vstage[:, j, :].rearrange("p (g e) -> p g e", g=4)),
                reads=[f"vst{j}"], writes=["vs"])
        for d in range(2):
            vec(lambda e, d=d: e.tensor_copy(out=ydup[:, :, :, d, :], in_=cstage[:].rearrange("p j (g e) -> p j g e", e=64)),
                reads=[f"cst{j}" for j in range(NS)], writes=["ydup"] + (YALL if d == 0 else []))
        for j in range(NS):
            def fp(e, j=j):
                ins = None
                for g in range(4):
                    ins = e.transpose(psb[:, g * 128:(g + 1) * 128], ydup[:, j, g].rearrange("p d e -> p (d e)"), ident_b[:])
                return ins
            pe(fp, reads=["ydup", "ident_b"], writes=["ps7"])
            vec(lambda e, j=j: e.tensor_copy(out=ksd[:, j, :, 0:127], in_=psb[:, 0:512].rearrange("p (g n) -> p g n", g=4)[:, :, 0:127]),
                reads=["ps7"], writes=["ksd"])

    def slope(hd):
        return float(2.0 ** (-(hd + 1) / 4.0))

    def bcol(t):
        return (0, NS) if t == TS else (NS, NS + MP)

    bch = {"i": 0}
    sring = {"i": 0}
    OB, DB = 3, 4

    def b_dest(tiles):
        i = bch["i"] % 2
        bch["i"] += 1
        d = {}
        for t in tiles:
            d[t] = (7, 500) if t == TS else (5, 0)
        return d

    def b_q_steps(p, j, g, tiles):
        wb = w_in_b[j]
        gb_ = g % 2
        st = {}

        def step_a(m):
            hp, mm = m // 2, m % 2
            if mm == 0:
                st["slot"] = load_std(wb, g * 512 + hp * 256)
            slot = st["slot"]
            c = 4 * g + m
            qb = chunk_matmul(slot, 16, mm * 128, h, hres, tiles, dest=b_dest(tiles), ksplit=(2 if (g == 0 and m == 0) else None))
            qr = qraw[c % 2]
            qrr = f"qraw{c % 2}"
            sq_ = sqb2[c % 2]
            sqr = f"sqb{c % 2}"

            def fq(e):
                ins = None
                for t in tiles:
                    b = bcol(t)
                    e.activation(out=qr[:, b[0]:b[1]], in_=pv(qb, t), func=AF.Copy)
                    ins = e.activation(out=sq_[:, b[0]:b[1]], in_=pv(qb, t), func=AF.Square)
                return ins
            act(fq, reads=pr(qb, tiles), writes=[qrr, sqr])

        def step_b(m):
            c = 4 * g + m
            qr = qraw[c % 2]
            qrr = f"qraw{c % 2}"
            sq_ = sqb2[c % 2]
            sqr = f"sqb{c % 2}"
            for t in tiles:
                b = bcol(t)
                n = t[1] - t[0]
                sbk = 7 if p == 1 else 6
                pe(lambda e, b=b, n=n, sbk=sbk: e.matmul(ps[sbk][:, 0:n], lhsT=blk64[:], rhs=sq_[:, b[0]:b[1]], start=True, stop=True),
                   reads=[sqr, "consts"], writes=[f"ps{sbk}"])
                act(lambda e, b=b, n=n, sbk=sbk: e.activation(out=rstd[:, b[0]:b[1]], in_=ps[sbk][:, 0:n], func=AF.Ln, bias=EPS, scale=1.0),
                    reads=[f"ps{sbk}"], writes=["rstd"])
            lo = bcol(tiles[0])[0]
            hi = NS + MP
            act(lambda e: e.activation(out=rstd[:, lo:hi], in_=rstd[:, lo:hi], func=AF.Exp, scale=-0.5),
                reads=["rstd"], writes=["rstd"])
            fns = [lambda e: e.scalar_tensor_tensor(out=qn[gb_][:, m, lo:hi], in0=qr[:, lo:hi], scalar=gq8[:, j:j + 1],
                                                    in1=rstd[:, lo:hi], op0=ALU.mult, op1=ALU.mult)]
            if p == 0:
                def fqs(e):
                    e.tensor_copy(out=qspad[0:64, gb_, 0, m, :], in_=qn[gb_][0:64, m, 0:NS])
                    return e.tensor_copy(out=qspad[64:128, gb_, 1, m, :], in_=qn[gb_][64:128, m, 0:NS])
                fns.append(fqs)
            chain("dve", fns, reads=["rstd", qrr, "gq8"], writes=[f"qn{gb_}", f"qspad{gb_}"])
        return [lambda m=m: step_a(m) for m in range(4)], [lambda m=m: step_b(m) for m in range(4)]

    def b_z_steps(p, j, g, tiles):
        wb = w_in_b[j]
        gb_ = g % 2
        st = {}

        def step(m):
            hp, mm = m // 2, m % 2
            if mm == 0:
                st["slot"] = load_std(wb, D + g * 512 + hp * 256)
            zb = chunk_matmul(st["slot"], 16, mm * 128, h, hres, tiles, dest=b_dest(tiles))

            def fz(e):
                ins = None
                for t in tiles:
                    b = bcol(t)
                    ins = e.activation(out=szB[gb_][:, m, b[0]:b[1]], in_=pv(zb, t), func=AF.Silu)
                return ins
            act(fz, reads=pr(zb, tiles), writes=[f"szB{gb_}"])
        return [lambda m=m: step(m) for m in range(4)]

    def b_att_steps(p, j, g):
        gb_ = g % 2
        qng, szg = qn[gb_], szB[gb_]
        qnr, szr = f"qn{gb_}", f"szB{gb_}"

        def emit_S(bi):
            q0 = NS + (bi - 1) * 128
            pt = pT[bi % 2]
            ptr = f"pT{bi % 2}"
            for kbi, kb in enumerate((bi - 1, bi)):
                if kbi == 1:
                    dsel = 0
                else:
                    dsel = 2 if (p == 0 and bi == 1) else 1
                for par in range(2):
                    sbl = [0, 1, 2, 6] if p == 1 else [0, 1, 2]
                    bk = sbl[sring["i"] % len(sbl)]
                    sring["i"] += 1
                    pe(lambda e, bk=bk, par=par, kb=kb: e.matmul(
                        ps[bk][:, :].rearrange("p (m n) -> p m n", m=4), lhsT=kTp[:, par, g, kb * 128:(kb + 1) * 128],
                        rhs=qng[:, :, q0:q0 + 128], start=True, stop=True),
                       reads=["kTp", qnr], writes=[f"ps{bk}"])
                    sbt = sbf[sring["i"] % 2]
                    sbr = f"sbf{sring['i'] % 2}"

                    def fb(e, bk=bk, par=par, dsel=dsel, sbt=sbt):
                        ins = None
                        for m in range(4):
                            hd = 2 * (4 * g + m) + par
                            ins = e.scalar_tensor_tensor(out=sbt[:, m * 128:(m + 1) * 128], in0=dist[:, dsel, :], scalar=-slope(hd),
                                                         in1=ps[bk][:, m * 128:(m + 1) * 128], op0=ALU.mult, op1=ALU.add)
                        return ins
                    vec(fb, reads=[f"ps{bk}", P], writes=[sbr])
                    act(lambda e, sbt=sbt, kbi=kbi, par=par: e.activation(out=pt[:, kbi, par, :], in_=sbt[:], func=AF.Exp,
                                                                          bias=negM[:, j:j + 1], scale=1.0),
                        reads=[sbr, "negM"], writes=[ptr])

        def emit_PV(bi):
            q0 = NS + (bi - 1) * 128
            hc0 = T1[0] + (bi - 1) * 128
            pt = pT[bi % 2]
            ptr = f"pT{bi % 2}"

            def fo(e):
                ins = None
                for (ob, use_v) in ((OB, True), (DB, False)):
                    i = 0
                    for kbi, kb in enumerate((bi - 1, bi)):
                        for par in range(2):
                            if use_v:
                                o = (64 if par == 0 else 0) + 128 * g
                                lt = vtm[:, kb, o:o + 128]
                            else:
                                lt = onespad[:, 64:192] if par == 0 else onespad[:, 0:128]
                            ins = e.matmul(ps[ob][:, :], lhsT=lt, rhs=pt[:, kbi, par, :], start=(i == 0), stop=(i == 3))
                            i += 1
                return ins
            pe(fo, reads=[ptr, "vtm", "consts"], writes=[f"ps{OB}", f"ps{DB}"])

            def f4a(e):
                ins = None
                for m in range(4):
                    ins = e.activation(out=t1[:, m * 128:(m + 1) * 128], in_=ps[DB][:, m * 128:(m + 1) * 128], func=AF.Ln,
                                       bias=esink[:, j, 4 * g + m:4 * g + m + 1], scale=1.0)
                return ins
            chain("act", [f4a, lambda e: e.activation(out=t1[:], in_=t1[:], func=AF.Exp, scale=-1.0)],
                  reads=[f"ps{DB}", "esink"], writes=["t1"])
            chain("dve", [
                lambda e: e.tensor_tensor(out=t1[:], in0=ps[OB][:, :], in1=t1[:], op=ALU.mult),
                lambda e: e.tensor_tensor(out=y[:, 4 * g:4 * g + 4, hc0:hc0 + 128], in0=t1[:].rearrange("p (m n) -> p m n", m=4),
                                          in1=szg[:, :, q0:q0 + 128], op=ALU.mult),
            ], reads=[f"ps{OB}", "t1", szr], writes=["t1"] + [yres(c, T1) for c in range(4 * g, 4 * g + 4)])

        def emit_samples(part):
            if part == 1:
                return emit_samples_b()

            def fss(e):
                ins = None
                for js in range(NS):
                    for par in range(2):
                        o = (js * 2 + par) * 4
                        ins = e.matmul(ps[7][:, o:o + 4], lhsT=ksd[:, js, g, :], rhs=qspad[:, gb_, par, :, js], start=True, stop=True)
                return ins
            pe(fss, reads=["ksd", f"qspad{gb_}"], writes=["ps7"])

            def fsb(e):
                return e.tensor_tensor(out=smallf[:, 0:32].rearrange("p (s a m) -> p s a m", s=NS, a=2),
                                       in0=ps[7][:, 0:32].rearrange("p (s a m) -> p s a m", s=NS, a=2),
                                       in1=sbias[:, :, 4 * g:4 * g + 4].unsqueeze(1).broadcast_to([128, NS, 2, 4]), op=ALU.add)
            vec(fsb, reads=["ps7", P], writes=["smallf"])
            act(lambda e: e.activation(out=pTs[:], in_=smallf[:, 0:32], func=AF.Exp, bias=negM[:, j:j + 1], scale=1.0),
                reads=["smallf", "negM"], writes=["pTs"])

        def emit_samples_b():
            def fso(e):
                ins = None
                for (off, use_v) in ((64, True), (96, False)):
                    for js in range(NS):
                        for par in range(2):
                            if use_v:
                                o2 = (64 if par == 0 else 0) + 128 * g
                                lt = vs[:, js, o2:o2 + 128]
                            else:
                                lt = onespad[:, 64:192] if par == 0 else onespad[:, 0:128]
                            o = (js * 2 + par) * 4
                            ins = e.matmul(ps[7][:, off + js * 4:off + js * 4 + 4], lhsT=lt, rhs=pTs[:, o:o + 4],
                                           start=(par == 0), stop=(par == 1))
                return ins
            pe(fso, reads=["pTs", "vs", "consts"], writes=["ps7"])
            tv = smallf[:, 32:48].rearrange("p (s m) -> p s m", s=NS)
            chain("dve", [
                lambda e: e.tensor_tensor(out=tv, in0=ps[7][:, 96:112].rearrange("p (s m) -> p s m", s=NS),
                                          in1=esink[:, j, 4 * g:4 * g + 4].unsqueeze(1).broadcast_to([128, NS, 4]), op=ALU.add),
                lambda e: e.reciprocal(out=smallf[:, 32:48], in_=smallf[:, 32:48]),
                lambda e: e.tensor_tensor(out=smallf[:, 32:48], in0=ps[7][:, 64:80], in1=smallf[:, 32:48], op=ALU.mult),
                lambda e: e.tensor_tensor(out=y[:, 4 * g:4 * g + 4, 0:NS], in0=smallf[:, 32:48].rearrange("p (s m) -> p m s", s=NS),
                                          in1=szg[:, :, 0:NS], op=ALU.mult),
            ], reads=["ps7", "esink", szr], writes=["smallf"] + [yres(c, TS) for c in range(4 * g, 4 * g + 4)])
        return emit_S, emit_PV, emit_samples

    def b_layer(p, j):
        tiles = [TS, T1] if p == 0 else [T1]
        rmsnorm(3 + j, tiles, mode=("reuse" if j == 0 else "stats_ready"))
        qa, qb_ = b_q_steps(p, j, 0, tiles)
        for m in range(4):
            qa[m]()
            if m > 0:
                qb_[m - 1]()
        qb_[3]()
        for z_ in b_z_steps(p, j, 0, tiles):
            z_()
        for g in range(4):
            S_, PV_, SM_ = b_att_steps(p, j, g)
            nop = [lambda: None] * 4
            qa, qb_ = b_q_steps(p, j, g + 1, tiles) if g < 3 else (nop, nop)
            zs = b_z_steps(p, j, g + 1, tiles) if g < 3 else []
            S_(1)
            qa[0]()
            S_(2)
            qa[1]()
            qb_[0]()
            PV_(1)
            qa[2]()
            qb_[1]()
            S_(3)
            qa[3]()
            qb_[2]()
            PV_(2)
            qb_[3]()
            S_(4)
            PV_(3)
            zl = list(zs)
            if zl:
                zl[0]()
            PV_(4)
            if p == 0:
                SM_(0)
            if len(zl) > 1:
                zl[1]()
            if p == 0:
                SM_(1)
            for z_ in zl[2:]:
                z_()
        w_out_phase(w_out_b[j], tiles, norm_tiles=(tiles if j == 0 else None))

    outs = []

    def store_y(p):
        jobs = [(T1[0] + i * 128, 128, y_main[p * MP + i * 128:p * MP + (i + 1) * 128, :]) for i in range(4)]
        if p == 0:
            jobs.append((0, NS, y_samp))
        for ji, (c0, n, dst) in enumerate(jobs):
            st, sr = stgs[ji % 2], STGR[ji % 2]
            for cq in range(4):
                bk = 6 + (cq % 2)

                def fp(e, cq=cq, bk=bk, c0=c0, n=n):
                    ins = None
                    for i in range(4):
                        ins = e.transpose(ps[bk][0:n, i * 128:(i + 1) * 128], xres[:, 4 * cq + i, c0:c0 + n], ident[:])
                    return ins
                pe(fp, reads=xall(ALLT) + [P], writes=[f"ps{bk}"])
                vec(lambda e, cq=cq, bk=bk, n=n, st=st: e.tensor_copy(out=st[0:n, cq * 512:(cq + 1) * 512], in_=ps[bk][0:n, :]),
                    reads=[f"ps{bk}"], writes=[sr] + (YALL if (ji == 0 and cq == 0) else []))
            outs.append(dma_sp(dst, st[0:n, :], reads=[sr], dkey=sr))

    def store_states(p):
        for l in range(2):
            st, sr = stgs[l % 2], STGR[l % 2]
            for cq in range(4):
                bk = 6 + (cq % 2)
                if p == 0:
                    def fp(e, cq=cq, bk=bk, l=l):
                        ins = None
                        for i in range(4):
                            ins = e.transpose(ps[bk][0:NS, i * 128:(i + 1) * 128], unew[:, l, 4 * cq + i, :], ident[:])
                        return ins
                    pe(fp, reads=["unew", P], writes=[f"ps{bk}"])
                    n = NS
                else:
                    def fp(e, cq=cq, bk=bk, l=l):
                        ins = None
                        for i in range(4):
                            ins = e.transpose(ps[bk][0:16, i * 128:(i + 1) * 128], ustate[:, l, 4 * cq + i, :], ident[:])
                        return ins
                    pe(fp, reads=["ustate", P], writes=[f"ps{bk}"])
                    n = 16
                vec(lambda e, cq=cq, bk=bk, n=n, st=st: e.tensor_copy(out=st[0:n, cq * 512:(cq + 1) * 512], in_=ps[bk][0:n, :]),
                    reads=[f"ps{bk}"], writes=[sr])
            if p == 0:
                outs.append(dma_sp(pool_s[l, :, 14, :], st[0:NS, :], reads=[sr], dkey=sr))
            else:
                outs.append(dma_sp(pool_p[l], st[1:16, :], reads=[sr], dkey=sr))

    for p in range(NPASS):
        if p == 0:
            load_caches()
        load_x(p)
        if p == 0:
            load_spool(0)
        a_layer(p, 0)
        a_layer(p, 1)
        kv_phase(p)
        b_layer(p, 0)
        b_layer(p, 1)
        store_y(p)
        store_states(p)

    eng_names = ["pe", "dve", "act", "pool", "sp"]
    cnt = {e: 0 for e in eng_names}
    for op in S.ops:
        if op.signal:
            cnt[op.eng] += 1
            op.sigval = cnt[op.eng]
    final_waits = {}
    for key, v in S.dcnt.items():
        final_waits[key] = v
    esem = {e: es.enter_context(nc.semaphore(f"sem_{e}")) for e in eng_names}
    dsem = {k: es.enter_context(nc.semaphore(f"dsem_{k}")) for k in S.dcnt}
    block = es.enter_context(nc.Block())

    def emit(engname):
        def body(e):
            waited = {}
            for op in S.ops:
                if op.eng != engname:
                    continue
                for d in op.deps:
                    if d.dkey is not None:
                        sem, val, key = dsem[d.dkey], d.dval, ("d", d.dkey)
                    else:
                        sem, val, key = esem[d.eng], d.sigval, ("e", d.eng)
                    if waited.get(key, 0) >= val:
                        continue
                    e.wait_ge(sem, val)
                    waited[key] = val
                ins = op.fn(e)
                if op.dkey is not None:
                    ins.then_inc(dsem[op.dkey], 16)
                elif op.signal:
                    ins.then_inc(esem[engname], 1)
            if engname == "sp":
                for key, v in final_waits.items():
                    e.wait_ge(dsem[key], v)
        return body

    block.tensor(emit("pe"))
    block.vector(emit("dve"))
    block.scalar(emit("act"))
    block.gpsimd(emit("pool"))
    block.sync(emit("sp"))
    es.close()
    return nc


_CACHE = {}


def _fm(v):
    return np.ascontiguousarray(np.asarray(v, np.float32).reshape(NCH, 128).T)


def kernel(x_prompt, x_sample, state_pool, cache_k_win, cache_v_win,
           norm_a, w_in_a, w_grp_a, scale_a, w_out_a,
           norm_kv, w_kv, k_norm,
           norm_b, w_in_b, q_norm, sinks, w_out_b):
    f = lambda a: np.ascontiguousarray(np.asarray(a, dtype=np.float32))
    x_prompt, x_sample, state_pool = f(x_prompt), f(x_sample), f(state_pool)
    cache_k_win, cache_v_win = f(cache_k_win), f(cache_v_win)
    w_in_a, w_grp_a, w_out_a, w_kv, w_in_b, w_out_b = map(f, (w_in_a, w_grp_a, w_out_a, w_kv, w_in_b, w_out_b))
    norm_a, scale_a, norm_kv, k_norm, norm_b, q_norm, sinks = map(f, (norm_a, scale_a, norm_kv, k_norm, norm_b, q_norm, sinks))

    if "nc" not in _CACHE:
        _CACHE["nc"] = build_program()
    nc = _CACHE["nc"]

    gains = np.stack([_fm(norm_a[0]), _fm(norm_a[1]), _fm(norm_kv), _fm(norm_b[0]), _fm(norm_b[1]),
                      _fm(scale_a[0]), _fm(scale_a[1])], axis=1).reshape(128, 7 * NCH)
    sinks_l = np.zeros((128, 2, NCH), np.float32)
    for j in range(2):
        sinks_l[0:64, j, :] = sinks[j, 0::2][None, :]
        sinks_l[64:128, j, :] = sinks[j, 1::2][None, :]
    sinks_l = sinks_l.reshape(128, 2 * NCH)
    gk_dup = np.concatenate([k_norm, k_norm]).reshape(128, 1)
    gq_dup = np.stack([np.concatenate([q_norm[0], q_norm[0]]), np.concatenate([q_norm[1], q_norm[1]])], axis=1)
    qk_rows = np.stack([q_norm[0], q_norm[1], k_norm], axis=0)
    ident = np.eye(128, dtype=np.float32)
    kk = np.arange(128)[:, None]
    qq = np.arange(128)[None, :]
    dcur = np.where(qq >= kk, (qq - kk).astype(np.float32), BIG).astype(np.float32)
    dprev = np.where(kk > qq, (qq - kk + 128).astype(np.float32), BIG).astype(np.float32)
    dbig = np.full((128, 128), BIG, np.float32)
    slopes = 2.0 ** (-(np.arange(32) + 1) / 4.0)
    sb_tab = np.zeros((128, 2, NCH), np.float32)
    for par in range(2):
        for c in range(NCH):
            sb_tab[:, par, c] = -slopes[2 * c + par] * (127 - np.arange(128))
    sb_tab = sb_tab.reshape(128, 2 * NCH)

    in_maps = []
    for core in range(NCORES):
        b, s = core // 4, core % 4
        t0 = s * MP * NPASS
        xcore = np.zeros((NROWS, D), np.float32)
        xcore[0:NS] = x_sample[core * NS:(core + 1) * NS, 0, :]
        if s > 0:
            xcore[NS:NS + HALO] = x_prompt[b, t0 - HALO:t0, :]
        xcore[NS + HALO:] = x_prompt[b, t0:t0 + MP * NPASS, :]
        invc = np.zeros((128, 4, 16), np.float32)
        for g in range(4):
            w = 2 << g
            if s == 0:
                invc[:, g, :] = (1.0 / np.minimum(np.arange(16) + 1, w))[None, :]
            else:
                invc[:, g, :] = 1.0 / w
        dist = np.stack([dcur, dprev, dbig if s == 0 else dprev], axis=1).reshape(128, 3 * 128)
        in_maps.append({
            "xc": xcore,
            "spool": np.ascontiguousarray(state_pool[:, core * NS:(core + 1) * NS].reshape(2, NS * 15, D)),
            "ck": np.ascontiguousarray(cache_k_win[core * NS:(core + 1) * NS].reshape(NS, 128, 256)),
            "cv": np.ascontiguousarray(cache_v_win[core * NS:(core + 1) * NS].reshape(NS, 128, 256)),
            "w_in_a": w_in_a, "w_grp_a": w_grp_a, "w_out_a": w_out_a, "w_kv": w_kv,
            "w_in_b": w_in_b, "w_out_b": w_out_b,
            "gains": np.ascontiguousarray(gains), "sinks_l": sinks_l, "gk_dup": np.ascontiguousarray(gk_dup),
            "gq_dup": np.ascontiguousarray(gq_dup), "qk_rows": np.ascontiguousarray(qk_rows),
            "ident": ident, "dist": np.ascontiguousarray(dist), "sbias": sb_tab,
            "invc": np.ascontiguousarray(invc.reshape(128, 64)),
        })

    res = run_bass_kernel_spmd(nc, in_maps, core_ids=list(range(NCORES)))
    R = res.results
    B, SEQ = x_prompt.shape[0], x_prompt.shape[1]
    y_prompt = np.zeros((B, SEQ, D), np.float32)
    y_sample = np.zeros((NCORES * NS, 1, D), np.float32)
    pool_p = np.zeros((2, B, 15, D), np.float32)
    pool_s = np.zeros((2, NCORES * NS, 15, D), np.float32)
    kwp = np.zeros((B, 128, 4, 64), np.float32)
    vwp = np.zeros((B, 128, 4, 64), np.float32)
    kws = np.zeros((NCORES * NS, 128, 4, 64), np.float32)
    vws = np.zeros((NCORES * NS, 128, 4, 64), np.float32)
    for core in range(NCORES):
        b, s = core // 4, core % 4
        r = R[core]
        y_prompt[b, s * 1024:(s + 1) * 1024] = r["y_main"]
        y_sample[core * NS:(core + 1) * NS, 0] = r["y_samp"]
        pool_s[:, core * NS:(core + 1) * NS] = r["pool_s"]
        kws[core * NS:(core + 1) * NS] = r["kwin_s"].reshape(NS, 128, 4, 64)
        vws[core * NS:(core + 1) * NS] = r["vwin_s"].reshape(NS, 128, 4, 64)
        if s == 3:
            pool_p[:, b] = r["pool_p"]
            kwp[b] = r["kwin_p"].reshape(128, 4, 64)
            vwp[b] = r["vwin_p"].reshape(128, 4, 64)
    return (y_prompt, y_sample, pool_p, pool_s, kwp, vwp, kws, vws)
```

```python
import contextlib
import numpy as np
import concourse.bass as bass
import concourse.mybir as mybir
from concourse.bass_utils import run_bass_kernel_spmd

F32, BF16 = mybir.dt.float32, mybir.dt.bfloat16
AF = mybir.ActivationFunctionType
ALU = mybir.AluOpType
AX = mybir.AxisListType

D = 2048
NCH = 16
NS = 4
HALO = 160
MP = 512
NPASS = 2
C = NS + HALO + MP
T0 = (0, NS + HALO)
T1 = (NS + HALO, C)
TS = (0, NS)
NROWS = NS + HALO + MP * NPASS
EPS = 1e-6
BIG = 1.0e9
NCORES = 8


class Res:
    __slots__ = ("name", "w", "r")

    def __init__(self, name):
        self.name = name
        self.w = None
        self.r = {}


class Op:
    __slots__ = ("eng", "fn", "deps", "signal", "sigval", "dkey", "dval")

    def __init__(self, eng, fn, dkey):
        self.eng = eng
        self.fn = fn
        self.deps = []
        self.signal = False
        self.sigval = 0
        self.dkey = dkey
        self.dval = 0


class Sched:
    def __init__(self):
        self.ops = []
        self.dcnt = {}
        self.res = {}

    def R(self, name):
        r = self.res.get(name)
        if r is None:
            r = self.res[name] = Res(name)
        return r

    def add(self, eng, fn, reads=(), writes=(), dkey=None):
        op = Op(eng, fn, dkey)
        deps = set()
        for r in reads:
            r = self.R(r) if isinstance(r, str) else r
            if r.w is not None:
                deps.add(r.w)
        wl = []
        for w in writes:
            w = self.R(w) if isinstance(w, str) else w
            wl.append(w)
            if w.w is not None:
                deps.add(w.w)
            deps.update(w.r.values())
        for d in deps:
            if d is op:
                continue
            if d.dkey is None and d.eng == eng and eng == "pe":
                continue
            op.deps.append(d)
            if d.dkey is None:
                d.signal = True
        if dkey is not None:
            self.dcnt[dkey] = self.dcnt.get(dkey, 0) + 16
            op.dval = self.dcnt[dkey]
        rk = eng if dkey is None else ("dma", dkey)
        for r in reads:
            r = self.R(r) if isinstance(r, str) else r
            r.r[rk] = op
        for w in wl:
            w.w = op
            w.r = {}
        self.ops.append(op)
        return op


def build_program():
    nc = bass.Bass("TRN2", target_bir_lowering=False)
    S = Sched()
    es = contextlib.ExitStack()

    def din(name, shape):
        return nc.dram_tensor(name, list(shape), F32, kind="ExternalInput").ap()

    def dout(name, shape):
        return nc.dram_tensor(name, list(shape), F32, kind="ExternalOutput").ap()

    xc = din("xc", [NROWS, D])
    spool = din("spool", [2, NS * 15, D])
    ck = din("ck", [NS, 128, 256])
    cv = din("cv", [NS, 128, 256])
    w_in_a = din("w_in_a", [2, D, 2 * D])
    w_grp_a = din("w_grp_a", [2, 4, 512, 512])
    w_out_a = din("w_out_a", [2, D, D])
    w_kv = din("w_kv", [D, 512])
    w_in_b = din("w_in_b", [2, D, 2 * D])
    w_out_b = din("w_out_b", [2, D, D])
    gains_d = din("gains", [128, 7 * NCH])
    sinks_d = din("sinks_l", [128, 2 * NCH])
    gkd_d = din("gk_dup", [128, 1])
    gqd_d = din("gq_dup", [128, 2])
    qk_d = din("qk_rows", [3, 64])
    ident_d = din("ident", [128, 128])
    dist_d = din("dist", [128, 3 * 128])
    sbias_d = din("sbias", [128, 2 * NCH])
    invc_d = din("invc", [128, 4 * 16])

    y_main = dout("y_main", [MP * NPASS, D])
    y_samp = dout("y_samp", [NS, D])
    pool_p = dout("pool_p", [2, 15, D])
    pool_s = dout("pool_s", [2, NS, 15, D])
    kwin_p = dout("kwin_p", [128, 256])
    vwin_p = dout("vwin_p", [128, 256])
    kwin_s = dout("kwin_s", [NS, 128, 256])
    vwin_s = dout("vwin_s", [NS, 128, 256])

    def sb(name, shape, dt):
        return es.enter_context(nc.sbuf_tensor(name, list(shape), dt))

    xres = sb("xres", [128, NCH, C], F32)
    h = sb("h", [128, NCH, C], BF16)
    y = sb("y", [128, NCH, C], BF16)
    wsl = [sb(f"wsl{i}", [128, 4096], BF16) for i in range(4)]
    kTp = sb("kTp", [128, 2, 4, 640], BF16)
    vtm = sb("vtm", [128, 5, 576], BF16)
    ksd = sb("ksd", [128, NS, 4, 128], BF16)
    vs = sb("vs", [128, NS, 576], BF16)
    scrf = sb("scrf", [128, 6696], F32)
    scrb = sb("scrb", [128, 8224], BF16)
    spoolT = sb("spoolT", [128, NCH, NS, 16], F32)
    unew = sb("unew", [128, 2, NCH, NS], F32)
    ustate = sb("ustate", [128, 2, NCH, 16], F32)
    rstd = sb("rstd", [128, C], F32)
    srt = rstd
    sqb = sb("sqb", [128, C], BF16)
    sqbB = sb("sqbB", [128, NS + MP], BF16)
    sqb2 = [sqb, sqbB]
    ident = sb("ident_s", [128, 128], F32)
    onesD = sb("onesD", [128, 128], BF16)
    ones128 = sb("ones128", [128, 128], BF16)
    blk64 = sb("blk64", [128, 128], BF16)
    onespad = sb("onespad", [128, 192], BF16)
    dist = sb("dist_s", [128, 3, 128], F32)
    sbias = sb("sbias_s", [128, 2, NCH], F32)
    invc = sb("invc_s", [128, 4, 16], F32)
    gains = sb("gains_s", [128, 7, NCH], F32)
    sinks = sb("sinks_s", [128, 2, NCH], F32)
    esink = sb("esink", [128, 2, NCH], F32)
    gkd = sb("gkd", [128, 1], F32)
    gq8 = sb("gq8", [128, 2], F32)
    qkb = sb("qkb", [128, 3, 64], F32)
    prod = sb("prod", [128, 64], F32)
    negM = sb("negM", [128, 2], F32)
    qspad = sb("qspad", [128, 2, 2, 4, NS], BF16)
    kout = sb("kout", [128, 256], F32)
    vout = sb("vout", [128, 256], F32)
    ktm = sb("ktm", [128, 256], F32)
    kss = sb("kss", [128, 8], F32)
    smallf = sb("smallf", [128, 64], F32)
    pTs = sb("pTs", [128, 32], BF16)
    cstage = sb("cstage", [128, NS, 256], BF16)
    vstage = sb("vstage", [128, NS, 256], BF16)
    ident_b = sb("ident_b", [128, 128], BF16)
    ps = [es.enter_context(nc.psum_tensor(f"ps{i}", [128, 512], F32)) for i in range(8)]
    _yflat = y[:].rearrange("p c n -> p (c n)")
    stgs = [_yflat[:, 0:4096].bitcast(F32), _yflat[:, 4096:8192].bitcast(F32)]
    STGR = ["stgA", "stgB", "ydup"]
    ydup = _yflat[:, 8192:8192 + NS * 512].rearrange("p (j g d e) -> p j g d e", j=NS, g=4, d=2)
    YALL = [f"y{c}_{t0}" for c in range(NCH) for t0 in (0, T1[0])]

    ubuf = [scrf[:, 0:688], scrf[:, 688:1376]]
    tmpb = [scrf[:, 1376:2064], scrf[:, 2064:2752]]
    szA = [scrf[:, 2752:3428], scrf[:, 3428:4104]]
    szB = [scrf[:, 0:2064].rearrange("p (m n) -> p m n", m=4), scrf[:, 2064:4128].rearrange("p (m n) -> p m n", m=4)]
    sbf = [scrf[:, 4128:4640], scrf[:, 4640:5152]]
    t1 = scrf[:, 5152:5664]
    qraw = [scrf[:, 5664:6180], scrf[:, 6180:6696]]
    kraw = scrf[:, 4128:4128 + C]
    krs = scrf[:, 4804:4804 + C]
    pbuf = [scrb[:, 0:2704].rearrange("p (m n) -> p m n", m=4),
            scrb[:, 2704:5408].rearrange("p (m n) -> p m n", m=4)]
    qn = [scrb[:, 0:2064].rearrange("p (m n) -> p m n", m=4), scrb[:, 2064:4128].rearrange("p (m n) -> p m n", m=4)]
    pT = [scrb[:, 4128:6176].rearrange("p (a b n) -> p a b n", a=2, b=2),
          scrb[:, 6176:8224].rearrange("p (a b n) -> p a b n", a=2, b=2)]
    SCR = []
    ksq = scrf[:, 5480:5736]
    knew = kout[0:NS, :]
    vnew = vout[0:NS, :]

    def vg(arr_ap):
        return arr_ap[:, 64:576].rearrange("p (g e) -> p g e", e=128)[:, :, 0:64]

    def wview(i, k, n):
        return wsl[i][:, 0:k * n].rearrange("p (k n) -> p k n", k=k)

    wfifo = [0, 1, 2, 3]

    def unpin(i):
        wfifo.append(i)

    def load_panel(src_aps, dst_fn, pinned=False):
        i = wfifo.pop(0)
        if not pinned:
            wfifo.append(i)
        for dv, sa in zip(dst_fn, src_aps):
            def fn(e, dv=dv, sa=sa, i=i):
                return e.dma_start(out=dv(i), in_=sa)
            S.add("pool", fn, reads=["xloaded"], writes=[f"wsl{i}"], dkey=f"wsl{i}")
        return i

    def load_std(wap, col0, ncol=256):
        src = wap.rearrange("(k p) n -> p k n", p=128)[:, :, col0:col0 + ncol]
        return load_panel([src], [lambda i: wview(i, 16, ncol)])

    def dma_sp(out, in_, reads=(), writes=(), dkey="par"):
        def fn(e):
            return e.dma_start(out=out, in_=in_)
        return S.add("sp", fn, reads=reads, writes=writes, dkey=dkey)

    P = "params"
    dma_sp(gains[:].rearrange("p a c -> p (a c)"), gains_d, writes=[P])
    dma_sp(sinks[:].rearrange("p a c -> p (a c)"), sinks_d, writes=[P])
    dma_sp(gkd[:], gkd_d, writes=[P])
    dma_sp(gq8[:], gqd_d, writes=[P])
    dma_sp(qkb[:].rearrange("p a c -> p (a c)"), qk_d.rearrange("a c -> (a c)").partition_broadcast(128), writes=[P])
    dma_sp(ident[:], ident_d, writes=[P])
    dma_sp(dist[:].rearrange("p a c -> p (a c)"), dist_d, writes=[P])
    dma_sp(sbias[:].rearrange("p a c -> p (a c)"), sbias_d, writes=[P])
    dma_sp(invc[:].rearrange("p a c -> p (a c)"), invc_d, writes=[P])

    def vec(fn, reads=(), writes=()):
        return S.add("dve", fn, reads=reads, writes=writes)

    def act(fn, reads=(), writes=()):
        return S.add("act", fn, reads=reads, writes=writes)

    def pe(fn, reads=(), writes=()):
        return S.add("pe", fn, reads=reads, writes=writes)

    def init_consts(e):
        e.memset(onesD[:], 1.0 / D)
        e.memset(ones128[:], 1.0 / 128)
        e.memset(blk64[:], 0.0)
        return e.memset(onespad[:], 0.0)

    def init_consts2(e):
        e.memset(blk64[0:64, 0:64], 1.0 / 64)
        e.memset(blk64[64:128, 64:128], 1.0 / 64)
        return e.memset(onespad[:, 64:128], 1.0)

    def init_zeros(e):
        ins = None
        for t_ in (scrf, scrb, kTp, vtm, vs, ksd, qspad, ustate, spoolT):
            ins = e.memzero(t_[:])
        return ins

    def init_cst(e):
        e.memzero(cstage[:])
        return e.memzero(vstage[:])
    act(init_cst, writes=[f"cst{j}" for j in range(NS)] + [f"vst{j}" for j in range(NS)])
    vec(init_consts, writes=["consts"])
    vec(lambda e: e.tensor_copy(out=ident_b[:], in_=ident[:]), reads=[P], writes=["ident_b"])
    vec(init_consts2, reads=["consts"], writes=["consts"])
    act(init_zeros, writes=["kTp", "vtm", "vs", "ksd", "qspad", "ubuf0", "ubuf1", "tmp0", "tmp1", "szA0", "szA1",
                            "pbuf0", "pbuf1", "ustate", "spoolT", "kraw", "qspad0", "qspad1"])

    for j in range(2):
        for fn in (lambda e, j=j: e.tensor_tensor(out=prod[:], in0=qkb[:, j, :], in1=qkb[:, 2, :], op=ALU.mult),
                   lambda e, j=j: e.tensor_reduce(out=negM[:, j:j + 1], in_=prod[:], axis=AX.X, op=ALU.max, apply_absolute_value=True),
                   lambda e, j=j: e.tensor_scalar(out=negM[:, j:j + 1], in0=negM[:, j:j + 1], scalar1=-8.0, scalar2=None, op0=ALU.mult)):
            vec(fn, reads=[P, "negM"], writes=["negM"])

        def fn2(e, j=j):
            return e.activation(out=esink[:, j, :], in_=sinks[:, j, :], func=AF.Exp, bias=negM[:, j:j + 1], scale=1.0)
        act(fn2, reads=[P, "negM"], writes=["esink"])
    vec(lambda e: e.tensor_scalar(out=gq8[:], in0=gq8[:], scalar1=0.125, scalar2=None, op0=ALU.mult),
        reads=[P], writes=["gq8"])

    bank_rr = {"i": 0}

    def next_pair():
        i = bank_rr["i"] % 3
        bank_rr["i"] += 1
        return 2 * i, 2 * i + 1

    def hres(c, t):
        return f"h{c}_{t[0]}"

    def xr(c, t):
        return f"x{c}_{t[0]}"

    def yres(c, t):
        return f"y{c}_{t[0]}"

    def pv(banks, t, a=None, b=None):
        bk, c0 = banks[t]
        n = t[1] - t[0]
        a = 0 if a is None else a
        b = n if b is None else b
        return ps[bk][:, c0 + a:c0 + b]

    def pr(banks, tiles):
        return [f"ps{banks[t][0]}" for t in tiles]

    def chunk_matmul(slot, wk, wcol, rhs_arr, rhs_res_fn, tiles, nk=NCH, kview=None, extra_reads=(), dest=None, ksplit=None):
        if dest is None:
            b0, b1 = next_pair()
            banks = {}
            for t in tiles:
                banks[t] = (b0 if (t[1] - t[0]) < 512 else b1, 0)
            if len(tiles) == 2 and banks[tiles[0]][0] == banks[tiles[1]][0]:
                banks[tiles[1]] = (b1, 0)
        else:
            banks = dest
        wv = kview if kview is not None else wview(slot, wk, 256)

        ks = ksplit if ksplit else nk
        for k0 in range(0, nk, ks):
            def fn(e, k0=k0):
                ins = None
                for k in range(k0, min(nk, k0 + ks)):
                    for t in tiles:
                        ins = e.matmul(pv(banks, t), lhsT=wv[:, k, wcol:wcol + 128],
                                       rhs=rhs_arr[:, k, t[0]:t[1]], start=(k == 0), stop=(k == nk - 1))
                return ins
            reads = [f"wsl{slot}"] + [rhs_res_fn(k, t) for k in range(k0, min(nk, k0 + ks)) for t in tiles] + list(extra_reads)
            pe(fn, reads=reads, writes=pr(banks, tiles))
        return banks

    def stat_dest(t):
        return (6, 0) if (t[1] - t[0]) == 512 else (7, 0)

    def emit_square(c, t):
        act(lambda e: e.activation(out=h[:, c, t[0]:t[1]], in_=xres[:, c, t[0]:t[1]], func=AF.Square),
            reads=[xr(c, t)], writes=[hres(c, t)])

    def emit_stat(c, t):
        bk, c0 = stat_dest(t)
        n = t[1] - t[0]
        pe(lambda e: e.matmul(ps[bk][:, c0:c0 + n], lhsT=onesD[:], rhs=h[:, c, t[0]:t[1]], start=(c == 0), stop=(c == NCH - 1)),
           reads=["consts", hres(c, t)], writes=[f"ps{bk}"])

    def rmsnorm(gi, tiles, mode="full"):
        if mode == "full":
            for t in tiles:
                for c in range(NCH):
                    emit_square(c, t)
                for c in range(NCH):
                    emit_stat(c, t)
        for t in tiles:
            n = t[1] - t[0]
            if mode != "reuse":
                bk, c0 = stat_dest(t)
                chain("act", [lambda e, t=t, n=n, bk=bk, c0=c0: e.activation(out=rstd[:, t[0]:t[1]], in_=ps[bk][:, c0:c0 + n], func=AF.Ln,
                                                                             bias=EPS, scale=1.0),
                              lambda e, t=t: e.activation(out=rstd[:, t[0]:t[1]], in_=rstd[:, t[0]:t[1]], func=AF.Exp, scale=-0.5)],
                      reads=[f"ps{bk}"], writes=["rstd"])
        for c in range(NCH):
            for t in tiles:
                def fh(e, c=c, t=t):
                    return e.scalar_tensor_tensor(out=h[:, c, t[0]:t[1]], in0=xres[:, c, t[0]:t[1]],
                                                  scalar=gains[:, gi, c:c + 1], in1=rstd[:, t[0]:t[1]],
                                                  op0=ALU.mult, op1=ALU.mult)
                vec(fh, reads=[xr(c, t), "rstd", P], writes=[hres(c, t)])

    def residual_add(banks, c, tiles):
        for t in tiles:
            n = t[1] - t[0]

            def fn(e, t=t, n=n):
                return e.tensor_tensor(out=xres[:, c, t[0]:t[1]], in0=pv(banks, t), in1=xres[:, c, t[0]:t[1]], op=ALU.add)
            vec(fn, reads=pr(banks, [t]) + [xr(c, t)], writes=[xr(c, t)])

    def w_out_phase(wap, tiles, norm_tiles=None, hook=None):
        pending = []
        for jj in range(8):
            slot = load_std(wap, jj * 256)
            if jj == 2 and hook is not None:
                hook()
            for mm in range(2):
                c = 2 * jj + mm
                banks = chunk_matmul(slot, 16, mm * 128, y, yres, tiles)
                residual_add(banks, c, tiles)
                if norm_tiles:
                    for t in norm_tiles:
                        emit_square(c, t)
                    pending.append(c)
                    if len(pending) > 2:
                        c2 = pending.pop(0)
                        for t in norm_tiles:
                            emit_stat(c2, t)
        for c2 in pending:
            for t in norm_tiles:
                emit_stat(c2, t)

    def xall(tiles):
        return [xr(c, t) for c in range(NCH) for t in tiles]

    def hall(tiles):
        return [hres(c, t) for c in range(NCH) for t in tiles]

    def yall(tiles):
        return [yres(c, t) for c in range(NCH) for t in tiles]

    ALLT = [T0, T1, TS]

    def load_x(p):
        if p == 0:
            row_tiles = [(r, min(r + 128, C)) for r in range(0, C, 128)]
            coff = 0
        else:
            row_tiles = [(C + r, C + r + 128) for r in range(0, MP, 128)]
            coff = T1[0] - C
        for ti, (r0, r1) in enumerate(row_tiles):
            nr = r1 - r0
            st = stgs[ti % 2]
            sr = STGR[ti % 2]
            dma_sp(st[0:nr, :], xc[r0:r1, :], writes=[sr] + (YALL if ti == 0 else []) + (["xloaded"] if (p == 0 and ti == 1) else []),
                   dkey=sr)
            for cq in range(4):
                bk = 6 + (cq % 2)

                def fp(e, cq=cq, bk=bk, st=st):
                    ins = None
                    for i in range(4):
                        c = 4 * cq + i
                        ins = e.transpose(ps[bk][:, i * 128:(i + 1) * 128], st[:, c * 128:(c + 1) * 128], ident[:])
                    return ins
                pe(fp, reads=[sr, P], writes=[f"ps{bk}"])
                c0 = r0 + coff

                def fv(e, cq=cq, bk=bk, nr=nr, c0=c0):
                    return e.tensor_copy(out=xres[:, 4 * cq:4 * cq + 4, c0:c0 + nr],
                                         in_=ps[bk][:].rearrange("p (c n) -> p c n", c=4)[:, :, 0:nr])
                vec(fv, reads=[f"ps{bk}"], writes=[xr(c, t) for c in range(4 * cq, 4 * cq + 4) for t in ALLT])

    spstg = scrf[:, 0:2048]
    SPG = ["ubuf0", "ubuf1", "tmp0"]

    def load_spool(l):
        dma_sp(spstg[0:NS * 15, :], spool[l], writes=["spstg"] + SPG, dkey="spstg")
        for cq in range(4):
            _, bk = next_pair()

            def fp(e, cq=cq, bk=bk):
                ins = None
                for i in range(4):
                    c = 4 * cq + i
                    ins = e.transpose(ps[bk][:, i * 128:(i + 1) * 128], spstg[:, c * 128:(c + 1) * 128], ident[:])
                return ins
            pe(fp, reads=["spstg", P], writes=[f"ps{bk}"])

            def fv(e, cq=cq, bk=bk):
                return e.tensor_copy(out=spoolT[:, 4 * cq:4 * cq + 4, :, 0:15],
                                     in_=ps[bk][:].rearrange("p (c n) -> p c n", c=4)[:, :, 0:60].rearrange("p c (j r) -> p c j r", j=NS))
            vec(fv, reads=[f"ps{bk}"], writes=["spoolT"])

        def fzero(e):
            e.memset(ubuf[0][:, 0:16], 0.0)
            return e.memset(ubuf[1][:, 0:16], 0.0)
        vec(fzero, reads=["spstg"], writes=["spstg"] + SPG)

    tokc = {"i": 0}

    def chain(eng, fns, reads=(), writes=()):
        tokc["i"] += 1
        tok = f"_tok{tokc['i']}"
        op = None
        for i, fn in enumerate(fns):
            rd = list(reads) + ([tok] if i > 0 else [])
            op = S.add(eng, fn, reads=rd, writes=list(writes) + [tok])
        return op

    def a_group(p, l, g, tiles):
        wa = w_in_a[l]
        w = 2 << g
        pb = pbuf[g % 2]
        pres = f"pbuf{g % 2}"
        for hp in range(2):
            slot = load_std(wa, g * 512 + hp * 256)
            for mm in range(2):
                m = 2 * hp + mm
                c = 4 * g + m
                banks = chunk_matmul(slot, 16, mm * 128, h, hres, tiles, ksplit=(2 if (g == 0 and hp == 0) else None))
                ub = ubuf[c % 2]
                ubr = f"ubuf{c % 2}"
                if p == 1:
                    vec(lambda e, ub=ub, c=c: e.tensor_copy(out=ub[:, 160:176], in_=ustate[:, l, c, :]),
                        reads=["ustate"], writes=[ubr])

                def fe(e, ub=ub, banks=banks, c=c):
                    ins = e.activation(out=ub[:, 176:688], in_=pv(banks, T1), func=AF.Copy)
                    if p == 0:
                        ins = e.activation(out=ub[:, 16:176], in_=pv(banks, T0, NS, NS + HALO), func=AF.Copy)
                        ins = e.activation(out=spoolT[:, c, :, 15], in_=pv(banks, T0, 0, NS), func=AF.Copy)
                        ins = e.activation(out=unew[:, l, c, :], in_=pv(banks, T0, 0, NS), func=AF.Copy)
                    return ins
                act(fe, reads=pr(banks, tiles), writes=[ubr, "spoolT", "unew"])
                vec(lambda e, ub=ub, c=c: e.tensor_copy(out=ustate[:, l, c, :], in_=ub[:, 672:688]),
                    reads=[ubr], writes=["ustate"])
                cur = ub
                curr = ubr
                step = 1
                ti = 0
                while step < w:
                    dst = tmpb[ti % 2]
                    dstr = f"tmp{ti % 2}"
                    lo = 2 * step - 1 + (160 if p == 1 else 0)

                    def fs(e, cur=cur, dst=dst, lo=lo, step=step):
                        return e.tensor_tensor(out=dst[:, lo:688], in0=cur[:, lo:688], in1=cur[:, lo - step:688 - step], op=ALU.add)
                    vec(fs, reads=[curr], writes=[dstr])
                    cur, curr = dst, dstr
                    step *= 2
                    ti += 1
                c0 = NS if p == 0 else T1[0]
                u0 = c0 + 12
                fns = [lambda e, cur=cur, ub=ub, m=m: e.scalar_tensor_tensor(
                    out=pb[:, m, c0:C], in0=cur[:, u0:688], scalar=1.0 / w, in1=ub[:, u0:688], op0=ALU.mult, op1=ALU.subtract)]
                if p == 0:
                    fns.append(lambda e, cur=cur: e.tensor_tensor(out=smallf[:, 0:16], in0=cur[:, 176:192], in1=invc[:, g, :], op=ALU.mult))
                    fns.append(lambda e, ub=ub, m=m: e.tensor_tensor(out=pb[:, m, T1[0]:T1[0] + 16], in0=smallf[:, 0:16],
                                                                      in1=ub[:, 176:192], op=ALU.subtract))
                    fns.append(lambda e, c=c: e.tensor_reduce(out=smallf[:, 16:16 + NS], in_=spoolT[:, c, :, 16 - w:16], axis=AX.X, op=ALU.add))
                    fns.append(lambda e, c=c, m=m: e.scalar_tensor_tensor(out=pb[:, m, 0:NS], in0=smallf[:, 16:16 + NS], scalar=1.0 / w,
                                                                         in1=spoolT[:, c, :, 15], op0=ALU.mult, op1=ALU.subtract))
                chain("dve", fns, reads=[curr, ubr, P, "spoolT"], writes=[pres, "smallf"])
        gsl = load_panel([w_grp_a[l, g].rearrange("(k p) n -> p k n", p=128)], [lambda i: wview(i, 4, 512)], pinned=True)
        gview = wview(gsl, 4, 512)
        for hp in range(2):
            slot = load_std(wa, D + g * 512 + hp * 256)
            zbs = []
            for mm in range(2):
                m = 2 * hp + mm
                c = 4 * g + m
                zb = chunk_matmul(slot, 16, mm * 128, h, hres, tiles)
                sz = szA[c % 2]
                szr = f"szA{c % 2}"

                def fz(e, zb=zb, sz=sz):
                    ins = None
                    for t in tiles:
                        ins = e.activation(out=sz[:, t[0]:t[1]], in_=pv(zb, t), func=AF.Silu)
                    return ins
                act(fz, reads=pr(zb, tiles), writes=[szr])
            for mm in range(2):
                m = 2 * hp + mm
                c = 4 * g + m
                sz = szA[c % 2]
                szr = f"szA{c % 2}"
                gb = chunk_matmul(gsl, 4, m * 128, pb, lambda k, t: pres, tiles, nk=4, kview=gview)

                def fy(e, gb=gb, sz=sz, c=c):
                    ins = None
                    for t in tiles:
                        ins = e.scalar_tensor_tensor(out=y[:, c, t[0]:t[1]], in0=pv(gb, t),
                                                     scalar=gains[:, 5 + l, c:c + 1], in1=sz[:, t[0]:t[1]],
                                                     op0=ALU.mult, op1=ALU.mult)
                    return ins
                vec(fy, reads=pr(gb, tiles) + [szr, P], writes=[yres(c, t) for t in tiles])
        unpin(gsl)

    def a_layer(p, l):
        tiles = [T0, T1] if p == 0 else [T1]
        rmsnorm(l, tiles, mode=("full" if l == 0 else "stats_ready"))
        if p == 0:
            for j in range(NS):
                dma_sp(pool_s[l, j, 0:14, :], spool[l, j * 15 + 1:j * 15 + 15, :], dkey="misc")
        for g in range(4):
            a_group(p, l, g, tiles)
        w_out_phase(w_out_a[l], tiles, norm_tiles=tiles,
                    hook=((lambda: load_spool(1)) if (p == 0 and l == 0) else None))

    def kv_kfm(p, g, slot, gi, tiles):
        kb = chunk_matmul(slot, 16, gi * 128, h, hres, tiles, ksplit=(2 if g == 0 else None))

        def fk(e):
            ins = None
            for t in tiles:
                n = t[1] - t[0]
                e.activation(out=kraw[:, t[0]:t[1]], in_=pv(kb, t), func=AF.Copy)
                ins = e.activation(out=sqb[:, t[0]:t[1]], in_=pv(kb, t), func=AF.Square)
            return ins
        act(fk, reads=pr(kb, tiles), writes=["kraw", "sqb"])
        for t in tiles:
            n = t[1] - t[0]
            pe(lambda e, t=t, n=n: e.matmul(ps[6][:, 0:n], lhsT=ones128[:], rhs=sqb[:, t[0]:t[1]], start=True, stop=True),
               reads=["sqb", "consts"], writes=["ps6"])
            chain("act", [lambda e, t=t, n=n: e.activation(out=krs[:, t[0]:t[1]], in_=ps[6][:, 0:n], func=AF.Ln, bias=EPS, scale=1.0),
                          lambda e, t=t: e.activation(out=krs[:, t[0]:t[1]], in_=krs[:, t[0]:t[1]], func=AF.Exp, scale=-0.5)],
                  reads=["ps6"], writes=["krs"])
        lo = tiles[0][0]
        fns = [lambda e: e.scalar_tensor_tensor(out=kraw[:, lo:C], in0=kraw[:, lo:C], scalar=gkd[:, 0:1], in1=krs[:, lo:C],
                                                op0=ALU.mult, op1=ALU.mult)]

        def fcp(e):
            e.tensor_copy(out=kTp[0:64, 0, g, 128:640], in_=kraw[0:64, T1[0]:T1[1]])
            ins = e.tensor_copy(out=kTp[64:128, 1, g, 128:640], in_=kraw[64:128, T1[0]:T1[1]])
            if p == 0:
                e.tensor_copy(out=kTp[0:64, 0, g, 0:128], in_=kraw[0:64, T0[1] - 128:T0[1]])
                e.tensor_copy(out=kTp[64:128, 1, g, 0:128], in_=kraw[64:128, T0[1] - 128:T0[1]])
                ins = e.tensor_copy(out=ksd[:, :, g, 127], in_=kraw[:, 0:NS])
            return ins
        fns.append(fcp)
        chain("dve", fns, reads=["krs", "kraw", P], writes=["kraw", "kTp", "ksd"])

    def tm_k(rows, c0, dst, dres, slk, kv_k, tiles):
        def fk(e):
            ins = None
            for k in range(NCH):
                ins = e.matmul(ps[7][0:rows, 256:512], lhsT=h[:, k, c0:c0 + rows], rhs=kv_k[:, k, :], start=(k == 0), stop=(k == NCH - 1))
            return ins
        pe(fk, reads=[f"wsl{slk}"] + hall(tiles), writes=["ps7"])
        g4 = lambda ap: ap.rearrange("p (g e) -> p g e", g=4)
        chain("dve", [
            lambda e: e.tensor_copy(out=ktm[0:rows, :], in_=ps[7][0:rows, 256:512]),
            lambda e: e.tensor_tensor(out=ksq[0:rows, :], in0=ktm[0:rows, :], in1=ktm[0:rows, :], op=ALU.mult),
            lambda e: e.tensor_reduce(out=kss[0:rows, 0:4], in_=g4(ksq[0:rows, :]), axis=AX.X, op=ALU.add),
        ], reads=["ps7"], writes=["ktm", "kss", "ksq"])
        act(lambda e: e.activation(out=kss[0:rows, 4:8], in_=kss[0:rows, 0:4], func=AF.Sqrt, bias=EPS, scale=1.0 / 64),
            reads=["kss"], writes=["kss2"])
        chain("dve", [
            lambda e: e.reciprocal(out=kss[0:rows, 0:4], in_=kss[0:rows, 4:8]),
            lambda e: e.tensor_tensor(out=g4(ktm[0:rows, :]), in0=g4(ktm[0:rows, :]),
                                      in1=kss[0:rows, 0:4].unsqueeze(2).broadcast_to([rows, 4, 64]), op=ALU.mult),
            lambda e: e.tensor_tensor(out=g4(dst), in0=g4(ktm[0:rows, :]),
                                      in1=qkb[0:rows, 2, :].unsqueeze(1).broadcast_to([rows, 4, 64]), op=ALU.mult),
        ], reads=["kss2", "ktm", P], writes=["kss", "ktm", dres])

    def kv_phase(p):
        tiles = [T0, T1] if p == 0 else [T1]
        rmsnorm(2, tiles, mode="stats_ready")
        wkv3 = w_kv.rearrange("(k p) n -> p k n", p=128)
        if p == 1:
            chain("dve", [lambda e: e.tensor_copy(out=kTp[:, :, :, 0:128], in_=kTp[:, :, :, 512:640]),
                          lambda e: e.tensor_copy(out=vtm[:, 0, :], in_=vtm[:, 4, :])],
                  reads=["kTp", "vtm"], writes=["kTp", "vtm"])
        for gh in range(2):
            srcs, dsts = [], []
            for d in range(2):
                for gi in range(2):
                    srcs.append(wkv3[:, :, gh * 128 + gi * 64:gh * 128 + gi * 64 + 64])
                    dsts.append(lambda i, d=d, gi=gi: wsl[i][:].rearrange("p (k g d e) -> p k g d e", k=16, g=2, d=2)[:, :, gi, d, :])
            slot = load_panel(srcs, dsts)
            for gi in range(2):
                kv_kfm(p, 2 * gh + gi, slot, gi, tiles)
        if p == 0:
            transpose_caches()
        slk = load_std(w_kv, 0)
        slv = load_std(w_kv, 256)
        kv_k = wview(slk, 16, 256)
        kv_v = wview(slv, 16, 256)
        blocks = []
        if p == 0:
            blocks.append((T0[1] - 128, 128, 0))
        for bi in range(4):
            blocks.append((T1[0] + bi * 128, 128, bi + 1))
        for vi, (c0, ncol, blk) in enumerate(blocks):
            vb = 7 if vi % 2 == 0 else 6

            def fv(e, c0=c0, ncol=ncol, vb=vb):
                ins = None
                for k in range(NCH):
                    ins = e.matmul(ps[vb][0:ncol, 0:256], lhsT=h[:, k, c0:c0 + ncol], rhs=kv_v[:, k, :], start=(k == 0), stop=(k == NCH - 1))
                return ins
            pe(fv, reads=[f"wsl{slv}"] + hall(tiles), writes=[f"ps{vb}"])
            last = (p == 1 and blk == 4)

            def fe(e, blk=blk, last=last, vb=vb):
                ins = e.tensor_copy(out=vg(vtm[:, blk, :]), in_=ps[vb][:, 0:256].rearrange("p (g e) -> p g e", g=4))
                if last:
                    ins = e.tensor_copy(out=vout[:], in_=ps[vb][:, 0:256])
                return ins
            vec(fe, reads=[f"ps{vb}"], writes=["vtm", "vout"])
            if last:
                outs.append(dma_sp(vwin_p, vout[:], reads=["vout"], dkey="vout"))
        if p == 1:
            tm_k(128, T1[1] - 128, kout[:], "kout", slk, kv_k, tiles)
            outs.append(dma_sp(kwin_p, kout[:], reads=["kout"], dkey="kout"))
        else:
            tm_k(NS, 0, knew, "kout", slk, kv_k, tiles)

            def fvs(e):
                ins = None
                for k in range(NCH):
                    ins = e.matmul(ps[7][0:NS, 0:256], lhsT=h[:, k, 0:NS], rhs=kv_v[:, k, :], start=(k == 0), stop=(k == NCH - 1))
                return ins
            pe(fvs, reads=[f"wsl{slv}"] + hall(tiles), writes=["ps7"])
            vec(lambda e: e.tensor_copy(out=vnew, in_=ps[7][0:NS, 0:256]), reads=["ps7"], writes=["vout"])
            for j in range(NS):
                outs.append(dma_sp(kwin_s[j, 127:128, :], knew[j:j + 1, :], reads=["kout"], dkey="kout"))
                outs.append(dma_sp(vwin_s[j, 127:128, :], vnew[j:j + 1, :], reads=["vout"], dkey="vout"))

                def fd(e, j=j):
                    return e.dma_start(out=vg(vs[127:128, j, :]), in_=vnew[j:j + 1, :].rearrange("p (g e) -> p g e", g=4))
                S.add("pool", fd, reads=["vout", "vs"], writes=["vs"], dkey="vs")

    def load_caches():
        for j in range(NS):
            S.add("pool", lambda e, j=j: e.dma_start(out=cstage[0:127, j, :], in_=ck[j, 1:128, :]),
                  writes=[f"cst{j}"], dkey=f"cst{j}")
            S.add("pool", lambda e, j=j: e.dma_start(out=vstage[0:127, j, :], in_=cv[j, 1:128, :]),
                  writes=[f"vst{j}"], dkey=f"vst{j}")
            dma_sp(kwin_s[j, 0:127, :], ck[j, 1:128, :], dkey="misc")
            dma_sp(vwin_s[j, 0:127, :], cv[j, 1:128, :], dkey="misc")

    def transpose_caches():
        psb = ps[7][:].bitcast(BF16)
        for j in range(NS):
            vec(lambda e, j=j: e.tensor_copy(out=vg(vs[:, j, :]), in_=vstage[:, j, :].rearrange("p (g e) -> p g e", g=4)),
                reads=[f"vst{j}"], writes=["vs"])
        for d in range(2):
            vec(lambda e, d=d: e.tensor_copy(out=ydup[:, :, :, d, :], in_=cstage[:].rearrange("p j (g e) -> p j g e", e=64)),
                reads=[f"cst{j}" for j in range(NS)], writes=["ydup"] + (YALL if d == 0 else []))
        for j in range(NS):
            def fp(e, j=j):
                ins = None
                for g in range(4):
                    ins = e.transpose(psb[:, g * 128:(g + 1) * 128], ydup[:, j, g].rearrange("p d e -> p (d e)"), ident_b[:])
                return ins
            pe(fp, reads=["ydup", "ident_b"], writes=["ps7"])
            vec(lambda e, j=j: e.tensor_copy(out=ksd[:, j, :, 0:127], in_=psb[:, 0:512].rearrange("p (g n) -> p g n", g=4)[:, :, 0:127]),
                reads=["ps7"], writes=["ksd"])

    def slope(hd):
        return float(2.0 ** (-(hd + 1) / 4.0))

    def bcol(t):
        return (0, NS) if t == TS else (NS, NS + MP)

    bch = {"i": 0}
    sring = {"i": 0}
    OB, DB = 3, 4

    def b_dest(tiles, alt=False):
        i = bch["i"] % 2
        bch["i"] += 1
        d = {}
        for t in tiles:
            d[t] = (7, 500) if t == TS else ((1 if (alt and i) else 5), 0)
        return d

    def b_q_steps(p, j, g, tiles):
        wb = w_in_b[j]
        gb_ = g % 2
        st = {}

        def step_a(m):
            hp, mm = m // 2, m % 2
            if mm == 0:
                st["slot"] = load_std(wb, g * 512 + hp * 256)
            slot = st["slot"]
            c = 4 * g + m
            qb = chunk_matmul(slot, 16, mm * 128, h, hres, tiles, dest=b_dest(tiles, alt=(g == 0)), ksplit=(2 if (g == 0 and m == 0) else None))
            qr = qraw[c % 2]
            qrr = f"qraw{c % 2}"
            sq_ = sqb2[c % 2]
            sqr = f"sqb{c % 2}"

            def fq(e):
                ins = None
                for t in tiles:
                    b = bcol(t)
                    e.activation(out=qr[:, b[0]:b[1]], in_=pv(qb, t), func=AF.Copy)
                    ins = e.activation(out=sq_[:, b[0]:b[1]], in_=pv(qb, t), func=AF.Square)
                return ins
            act(fq, reads=pr(qb, tiles), writes=[qrr, sqr])

        def step_b(m):
            c = 4 * g + m
            qr = qraw[c % 2]
            qrr = f"qraw{c % 2}"
            sq_ = sqb2[c % 2]
            sqr = f"sqb{c % 2}"
            for t in tiles:
                b = bcol(t)
                n = t[1] - t[0]
                sbk = 7 if p == 1 else 6
                pe(lambda e, b=b, n=n, sbk=sbk: e.matmul(ps[sbk][:, 0:n], lhsT=blk64[:], rhs=sq_[:, b[0]:b[1]], start=True, stop=True),
                   reads=[sqr, "consts"], writes=[f"ps{sbk}"])
                act(lambda e, b=b, n=n, sbk=sbk: e.activation(out=rstd[:, b[0]:b[1]], in_=ps[sbk][:, 0:n], func=AF.Ln, bias=EPS, scale=1.0),
                    reads=[f"ps{sbk}"], writes=["rstd"])
            lo = bcol(tiles[0])[0]
            hi = NS + MP
            act(lambda e: e.activation(out=rstd[:, lo:hi], in_=rstd[:, lo:hi], func=AF.Exp, scale=-0.5),
                reads=["rstd"], writes=["rstd"])
            fns = [lambda e: e.scalar_tensor_tensor(out=qn[gb_][:, m, lo:hi], in0=qr[:, lo:hi], scalar=gq8[:, j:j + 1],
                                                    in1=rstd[:, lo:hi], op0=ALU.mult, op1=ALU.mult)]
            if p == 0:
                def fqs(e):
                    e.tensor_copy(out=qspad[0:64, gb_, 0, m, :], in_=qn[gb_][0:64, m, 0:NS])
                    return e.tensor_copy(out=qspad[64:128, gb_, 1, m, :], in_=qn[gb_][64:128, m, 0:NS])
                fns.append(fqs)
            chain("dve", fns, reads=["rstd", qrr, "gq8"], writes=[f"qn{gb_}", f"qspad{gb_}"])
        return [lambda m=m: step_a(m) for m in range(4)], [lambda m=m: step_b(m) for m in range(4)]

    def b_z_steps(p, j, g, tiles):
        wb = w_in_b[j]
        gb_ = g % 2
        st = {}

        def step(m):
            hp, mm = m // 2, m % 2
            if mm == 0:
                st["slot"] = load_std(wb, D + g * 512 + hp * 256)
            zb = chunk_matmul(st["slot"], 16, mm * 128, h, hres, tiles, dest=b_dest(tiles, alt=(g == 0)))

            def fz(e):
                ins = None
                for t in tiles:
                    b = bcol(t)
                    ins = e.activation(out=szB[gb_][:, m, b[0]:b[1]], in_=pv(zb, t), func=AF.Silu)
                return ins
            act(fz, reads=pr(zb, tiles), writes=[f"szB{gb_}"])
        return [lambda m=m: step(m) for m in range(4)]

    def b_att_steps(p, j, g):
        gb_ = g % 2
        qng, szg = qn[gb_], szB[gb_]
        qnr, szr = f"qn{gb_}", f"szB{gb_}"

        def emit_S(bi):
            q0 = NS + (bi - 1) * 128
            pt = pT[bi % 2]
            ptr = f"pT{bi % 2}"
            for kbi, kb in enumerate((bi - 1, bi)):
                if kbi == 1:
                    dsel = 0
                else:
                    dsel = 2 if (p == 0 and bi == 1) else 1
                for par in range(2):
                    sbl = [0, 1, 2, 6] if p == 1 else [0, 1, 2]
                    bk = sbl[sring["i"] % len(sbl)]
                    sring["i"] += 1
                    pe(lambda e, bk=bk, par=par, kb=kb: e.matmul(
                        ps[bk][:, :].rearrange("p (m n) -> p m n", m=4), lhsT=kTp[:, par, g, kb * 128:(kb + 1) * 128],
                        rhs=qng[:, :, q0:q0 + 128], start=True, stop=True),
                       reads=["kTp", qnr], writes=[f"ps{bk}"])
                    sbt = sbf[sring["i"] % 2]
                    sbr = f"sbf{sring['i'] % 2}"

                    def fb(e, bk=bk, par=par, dsel=dsel, sbt=sbt):
                        ins = None
                        for m in range(4):
                            hd = 2 * (4 * g + m) + par
                            ins = e.scalar_tensor_tensor(out=sbt[:, m * 128:(m + 1) * 128], in0=dist[:, dsel, :], scalar=-slope(hd),
                                                         in1=ps[bk][:, m * 128:(m + 1) * 128], op0=ALU.mult, op1=ALU.add)
                        return ins
                    vec(fb, reads=[f"ps{bk}", P], writes=[sbr])
                    act(lambda e, sbt=sbt, kbi=kbi, par=par: e.activation(out=pt[:, kbi, par, :], in_=sbt[:], func=AF.Exp,
                                                                          bias=negM[:, j:j + 1], scale=1.0),
                        reads=[sbr, "negM"], writes=[ptr])

        def emit_PV(bi):
            q0 = NS + (bi - 1) * 128
            hc0 = T1[0] + (bi - 1) * 128
            pt = pT[bi % 2]
            ptr = f"pT{bi % 2}"

            def fo(e):
                ins = None
                for (ob, use_v) in ((OB, True), (DB, False)):
                    i = 0
                    for kbi, kb in enumerate((bi - 1, bi)):
                        for par in range(2):
                            if use_v:
                                o = (64 if par == 0 else 0) + 128 * g
                                lt = vtm[:, kb, o:o + 128]
                            else:
                                lt = onespad[:, 64:192] if par == 0 else onespad[:, 0:128]
                            ins = e.matmul(ps[ob][:, :], lhsT=lt, rhs=pt[:, kbi, par, :], start=(i == 0), stop=(i == 3))
                            i += 1
                return ins
            pe(fo, reads=[ptr, "vtm", "consts"], writes=[f"ps{OB}", f"ps{DB}"])

            def f4a(e):
                ins = None
                for m in range(4):
                    ins = e.activation(out=t1[:, m * 128:(m + 1) * 128], in_=ps[DB][:, m * 128:(m + 1) * 128], func=AF.Ln,
                                       bias=esink[:, j, 4 * g + m:4 * g + m + 1], scale=1.0)
                return ins
            chain("act", [f4a, lambda e: e.activation(out=t1[:], in_=t1[:], func=AF.Exp, scale=-1.0)],
                  reads=[f"ps{DB}", "esink"], writes=["t1"])
            chain("dve", [
                lambda e: e.tensor_tensor(out=t1[:], in0=ps[OB][:, :], in1=t1[:], op=ALU.mult),
                lambda e: e.tensor_tensor(out=y[:, 4 * g:4 * g + 4, hc0:hc0 + 128], in0=t1[:].rearrange("p (m n) -> p m n", m=4),
                                          in1=szg[:, :, q0:q0 + 128], op=ALU.mult),
            ], reads=[f"ps{OB}", "t1", szr], writes=["t1"] + [yres(c, T1) for c in range(4 * g, 4 * g + 4)])

        def emit_samples(part):
            if part == 1:
                return emit_samples_b()

            def fss(e):
                ins = None
                for js in range(NS):
                    for par in range(2):
                        o = (js * 2 + par) * 4
                        ins = e.matmul(ps[7][:, o:o + 4], lhsT=ksd[:, js, g, :], rhs=qspad[:, gb_, par, :, js], start=True, stop=True)
                return ins
            pe(fss, reads=["ksd", f"qspad{gb_}"], writes=["ps7"])

            def fsb(e):
                return e.tensor_tensor(out=smallf[:, 0:32].rearrange("p (s a m) -> p s a m", s=NS, a=2),
                                       in0=ps[7][:, 0:32].rearrange("p (s a m) -> p s a m", s=NS, a=2),
                                       in1=sbias[:, :, 4 * g:4 * g + 4].unsqueeze(1).broadcast_to([128, NS, 2, 4]), op=ALU.add)
            vec(fsb, reads=["ps7", P], writes=["smallf"])
            act(lambda e: e.activation(out=pTs[:], in_=smallf[:, 0:32], func=AF.Exp, bias=negM[:, j:j + 1], scale=1.0),
                reads=["smallf", "negM"], writes=["pTs"])

        def emit_samples_b():
            def fso(e):
                ins = None
                for (off, use_v) in ((64, True), (96, False)):
                    for js in range(NS):
                        for par in range(2):
                            if use_v:
                                o2 = (64 if par == 0 else 0) + 128 * g
                                lt = vs[:, js, o2:o2 + 128]
                            else:
                                lt = onespad[:, 64:192] if par == 0 else onespad[:, 0:128]
                            o = (js * 2 + par) * 4
                            ins = e.matmul(ps[7][:, off + js * 4:off + js * 4 + 4], lhsT=lt, rhs=pTs[:, o:o + 4],
                                           start=(par == 0), stop=(par == 1))
                return ins
            pe(fso, reads=["pTs", "vs", "consts"], writes=["ps7"])
            tv = smallf[:, 32:48].rearrange("p (s m) -> p s m", s=NS)
            chain("dve", [
                lambda e: e.tensor_tensor(out=tv, in0=ps[7][:, 96:112].rearrange("p (s m) -> p s m", s=NS),
                                          in1=esink[:, j, 4 * g:4 * g + 4].unsqueeze(1).broadcast_to([128, NS, 4]), op=ALU.add),
                lambda e: e.reciprocal(out=smallf[:, 32:48], in_=smallf[:, 32:48]),
                lambda e: e.tensor_tensor(out=smallf[:, 32:48], in0=ps[7][:, 64:80], in1=smallf[:, 32:48], op=ALU.mult),
                lambda e: e.tensor_tensor(out=y[:, 4 * g:4 * g + 4, 0:NS], in0=smallf[:, 32:48].rearrange("p (s m) -> p m s", s=NS),
                                          in1=szg[:, :, 0:NS], op=ALU.mult),
            ], reads=["ps7", "esink", szr], writes=["smallf"] + [yres(c, TS) for c in range(4 * g, 4 * g + 4)])
        return emit_S, emit_PV, emit_samples

    def b_layer(p, j):
        tiles = [TS, T1] if p == 0 else [T1]
        rmsnorm(3 + j, tiles, mode=("reuse" if j == 0 else "stats_ready"))
        qa, qb_ = b_q_steps(p, j, 0, tiles)
        for m in range(4):
            qa[m]()
            if m > 0:
                qb_[m - 1]()
        qb_[3]()
        for z_ in b_z_steps(p, j, 0, tiles):
            z_()
        for g in range(4):
            S_, PV_, SM_ = b_att_steps(p, j, g)
            nop = [lambda: None] * 4
            qa, qb_ = b_q_steps(p, j, g + 1, tiles) if g < 3 else (nop, nop)
            zs = b_z_steps(p, j, g + 1, tiles) if g < 3 else []
            S_(1)
            qa[0]()
            S_(2)
            qa[1]()
            qb_[0]()
            PV_(1)
            qa[2]()
            qb_[1]()
            S_(3)
            qa[3]()
            qb_[2]()
            PV_(2)
            qb_[3]()
            S_(4)
            PV_(3)
            zl = list(zs)
            if zl:
                zl[0]()
            PV_(4)
            if p == 0:
                SM_(0)
            if len(zl) > 1:
                zl[1]()
            if p == 0:
                SM_(1)
            for z_ in zl[2:]:
                z_()
        w_out_phase(w_out_b[j], tiles, norm_tiles=(tiles if j == 0 else None))

    outs = []

    def store_y(p):
        jobs = [(T1[0] + i * 128, 128, y_main[p * MP + i * 128:p * MP + (i + 1) * 128, :]) for i in range(4)]
        if p == 0:
            jobs.append((0, NS, y_samp))
        for ji, (c0, n, dst) in enumerate(jobs):
            st, sr = stgs[ji % 2], STGR[ji % 2]
            for cq in range(4):
                bk = 6 + (cq % 2)

                def fp(e, cq=cq, bk=bk, c0=c0, n=n):
                    ins = None
                    for i in range(4):
                        ins = e.transpose(ps[bk][0:n, i * 128:(i + 1) * 128], xres[:, 4 * cq + i, c0:c0 + n], ident[:])
                    return ins
                pe(fp, reads=xall(ALLT) + [P], writes=[f"ps{bk}"])
                vec(lambda e, cq=cq, bk=bk, n=n, st=st: e.tensor_copy(out=st[0:n, cq * 512:(cq + 1) * 512], in_=ps[bk][0:n, :]),
                    reads=[f"ps{bk}"], writes=[sr] + (YALL if (ji == 0 and cq == 0) else []))
            outs.append(dma_sp(dst, st[0:n, :], reads=[sr], dkey=sr))

    def store_states(p):
        for l in range(2):
            st, sr = stgs[l % 2], STGR[l % 2]
            for cq in range(4):
                bk = 6 + (cq % 2)
                if p == 0:
                    def fp(e, cq=cq, bk=bk, l=l):
                        ins = None
                        for i in range(4):
                            ins = e.transpose(ps[bk][0:NS, i * 128:(i + 1) * 128], unew[:, l, 4 * cq + i, :], ident[:])
                        return ins
                    pe(fp, reads=["unew", P], writes=[f"ps{bk}"])
                    n = NS
                else:
                    def fp(e, cq=cq, bk=bk, l=l):
                        ins = None
                        for i in range(4):
                            ins = e.transpose(ps[bk][0:16, i * 128:(i + 1) * 128], ustate[:, l, 4 * cq + i, :], ident[:])
                        return ins
                    pe(fp, reads=["ustate", P], writes=[f"ps{bk}"])
                    n = 16
                vec(lambda e, cq=cq, bk=bk, n=n, st=st: e.tensor_copy(out=st[0:n, cq * 512:(cq + 1) * 512], in_=ps[bk][0:n, :]),
                    reads=[f"ps{bk}"], writes=[sr])
            if p == 0:
                outs.append(dma_sp(pool_s[l, :, 14, :], st[0:NS, :], reads=[sr], dkey=sr))
            else:
                outs.append(dma_sp(pool_p[l], st[1:16, :], reads=[sr], dkey=sr))

    for p in range(NPASS):
        if p == 0:
            load_caches()
        load_x(p)
        if p == 0:
            load_spool(0)
        a_layer(p, 0)
        a_layer(p, 1)
        kv_phase(p)
        b_layer(p, 0)
        b_layer(p, 1)
        store_y(p)
        store_states(p)

    eng_names = ["pe", "dve", "act", "pool", "sp"]
    cnt = {e: 0 for e in eng_names}
    for op in S.ops:
        if op.signal:
            cnt[op.eng] += 1
            op.sigval = cnt[op.eng]
    final_waits = {}
    for key, v in S.dcnt.items():
        final_waits[key] = v
    esem = {e: es.enter_context(nc.semaphore(f"sem_{e}")) for e in eng_names}
    dsem = {k: es.enter_context(nc.semaphore(f"dsem_{k}")) for k in S.dcnt}
    block = es.enter_context(nc.Block())

    def emit(engname):
        def body(e):
            waited = {}
            for op in S.ops:
                if op.eng != engname:
                    continue
                for d in op.deps:
                    if d.dkey is not None:
                        sem, val, key = dsem[d.dkey], d.dval, ("d", d.dkey)
                    else:
                        sem, val, key = esem[d.eng], d.sigval, ("e", d.eng)
                    if waited.get(key, 0) >= val:
                        continue
                    e.wait_ge(sem, val)
                    waited[key] = val
                ins = op.fn(e)
                if op.dkey is not None:
                    ins.then_inc(dsem[op.dkey], 16)
                elif op.signal:
                    ins.then_inc(esem[engname], 1)
            if engname == "sp":
                for key, v in final_waits.items():
                    e.wait_ge(dsem[key], v)
        return body

    block.tensor(emit("pe"))
    block.vector(emit("dve"))
    block.scalar(emit("act"))
    block.gpsimd(emit("pool"))
    block.sync(emit("sp"))
    es.close()
    return nc


_CACHE = {}


def _fm(v):
    return np.ascontiguousarray(np.asarray(v, np.float32).reshape(NCH, 128).T)


def kernel(x_prompt, x_sample, state_pool, cache_k_win, cache_v_win,
           norm_a, w_in_a, w_grp_a, scale_a, w_out_a,
           norm_kv, w_kv, k_norm,
           norm_b, w_in_b, q_norm, sinks, w_out_b):
    f = lambda a: np.ascontiguousarray(np.asarray(a, dtype=np.float32))
    x_prompt, x_sample, state_pool = f(x_prompt), f(x_sample), f(state_pool)
    cache_k_win, cache_v_win = f(cache_k_win), f(cache_v_win)
    w_in_a, w_grp_a, w_out_a, w_kv, w_in_b, w_out_b = map(f, (w_in_a, w_grp_a, w_out_a, w_kv, w_in_b, w_out_b))
    norm_a, scale_a, norm_kv, k_norm, norm_b, q_norm, sinks = map(f, (norm_a, scale_a, norm_kv, k_norm, norm_b, q_norm, sinks))

    if "nc" not in _CACHE:
        _CACHE["nc"] = build_program()
    nc = _CACHE["nc"]

    gains = np.stack([_fm(norm_a[0]), _fm(norm_a[1]), _fm(norm_kv), _fm(norm_b[0]), _fm(norm_b[1]),
                      _fm(scale_a[0]), _fm(scale_a[1])], axis=1).reshape(128, 7 * NCH)
    sinks_l = np.zeros((128, 2, NCH), np.float32)
    for j in range(2):
        sinks_l[0:64, j, :] = sinks[j, 0::2][None, :]
        sinks_l[64:128, j, :] = sinks[j, 1::2][None, :]
    sinks_l = sinks_l.reshape(128, 2 * NCH)
    gk_dup = np.concatenate([k_norm, k_norm]).reshape(128, 1)
    gq_dup = np.stack([np.concatenate([q_norm[0], q_norm[0]]), np.concatenate([q_norm[1], q_norm[1]])], axis=1)
    qk_rows = np.stack([q_norm[0], q_norm[1], k_norm], axis=0)
    ident = np.eye(128, dtype=np.float32)
    kk = np.arange(128)[:, None]
    qq = np.arange(128)[None, :]
    dcur = np.where(qq >= kk, (qq - kk).astype(np.float32), BIG).astype(np.float32)
    dprev = np.where(kk > qq, (qq - kk + 128).astype(np.float32), BIG).astype(np.float32)
    dbig = np.full((128, 128), BIG, np.float32)
    slopes = 2.0 ** (-(np.arange(32) + 1) / 4.0)
    sb_tab = np.zeros((128, 2, NCH), np.float32)
    for par in range(2):
        for c in range(NCH):
            sb_tab[:, par, c] = -slopes[2 * c + par] * (127 - np.arange(128))
    sb_tab = sb_tab.reshape(128, 2 * NCH)

    in_maps = []
    for core in range(NCORES):
        b, s = core // 4, core % 4
        t0 = s * MP * NPASS
        xcore = np.zeros((NROWS, D), np.float32)
        xcore[0:NS] = x_sample[core * NS:(core + 1) * NS, 0, :]
        if s > 0:
            xcore[NS:NS + HALO] = x_prompt[b, t0 - HALO:t0, :]
        xcore[NS + HALO:] = x_prompt[b, t0:t0 + MP * NPASS, :]
        invc = np.zeros((128, 4, 16), np.float32)
        for g in range(4):
            w = 2 << g
            if s == 0:
                invc[:, g, :] = (1.0 / np.minimum(np.arange(16) + 1, w))[None, :]
            else:
                invc[:, g, :] = 1.0 / w
        dist = np.stack([dcur, dprev, dbig if s == 0 else dprev], axis=1).reshape(128, 3 * 128)
        in_maps.append({
            "xc": xcore,
            "spool": np.ascontiguousarray(state_pool[:, core * NS:(core + 1) * NS].reshape(2, NS * 15, D)),
            "ck": np.ascontiguousarray(cache_k_win[core * NS:(core + 1) * NS].reshape(NS, 128, 256)),
            "cv": np.ascontiguousarray(cache_v_win[core * NS:(core + 1) * NS].reshape(NS, 128, 256)),
            "w_in_a": w_in_a, "w_grp_a": w_grp_a, "w_out_a": w_out_a, "w_kv": w_kv,
            "w_in_b": w_in_b, "w_out_b": w_out_b,
            "gains": np.ascontiguousarray(gains), "sinks_l": sinks_l, "gk_dup": np.ascontiguousarray(gk_dup),
            "gq_dup": np.ascontiguousarray(gq_dup), "qk_rows": np.ascontiguousarray(qk_rows),
            "ident": ident, "dist": np.ascontiguousarray(dist), "sbias": sb_tab,
            "invc": np.ascontiguousarray(invc.reshape(128, 64)),
        })

    res = run_bass_kernel_spmd(nc, in_maps, core_ids=list(range(NCORES)))
    R = res.results
    B, SEQ = x_prompt.shape[0], x_prompt.shape[1]
    y_prompt = np.zeros((B, SEQ, D), np.float32)
    y_sample = np.zeros((NCORES * NS, 1, D), np.float32)
    pool_p = np.zeros((2, B, 15, D), np.float32)
    pool_s = np.zeros((2, NCORES * NS, 15, D), np.float32)
    kwp = np.zeros((B, 128, 4, 64), np.float32)
    vwp = np.zeros((B, 128, 4, 64), np.float32)
    kws = np.zeros((NCORES * NS, 128, 4, 64), np.float32)
    vws = np.zeros((NCORES * NS, 128, 4, 64), np.float32)
    for core in range(NCORES):
        b, s = core // 4, core % 4
        r = R[core]
        y_prompt[b, s * 1024:(s + 1) * 1024] = r["y_main"]
        y_sample[core * NS:(core + 1) * NS, 0] = r["y_samp"]
        pool_s[:, core * NS:(core + 1) * NS] = r["pool_s"]
        kws[core * NS:(core + 1) * NS] = r["kwin_s"].reshape(NS, 128, 4, 64)
        vws[core * NS:(core + 1) * NS] = r["vwin_s"].reshape(NS, 128, 4, 64)
        if s == 3:
            pool_p[:, b] = r["pool_p"]
            kwp[b] = r["kwin_p"].reshape(128, 4, 64)
            vwp[b] = r["vwin_p"].reshape(128, 4, 64)
    return (y_prompt, y_sample, pool_p, pool_s, kwp, vwp, kws, vws)
```

```python
import contextlib
import numpy as np
import concourse.bass as bass
import concourse.mybir as mybir
from concourse.bass_utils import run_bass_kernel_spmd

F32, BF16 = mybir.dt.float32, mybir.dt.bfloat16
AF = mybir.ActivationFunctionType
ALU = mybir.AluOpType
AX = mybir.AxisListType

D = 2048
NCH = 16
NS = 4
HALO = 160
MP = 512
NPASS = 2
C = NS + HALO + MP
T0 = (0, NS + HALO)
T1 = (NS + HALO, C)
TS = (0, NS)
NROWS = NS + HALO + MP * NPASS
EPS = 1e-6
BIG = 1.0e9
NCORES = 8


class Res:
    __slots__ = ("name", "w", "r")

    def __init__(self, name):
        self.name = name
        self.w = None
        self.r = {}


class Op:
    __slots__ = ("eng", "fn", "deps", "signal", "sigval", "dkey", "dval")

    def __init__(self, eng, fn, dkey):
        self.eng = eng
        self.fn = fn
        self.deps = []
        self.signal = False
        self.sigval = 0
        self.dkey = dkey
        self.dval = 0


class Sched:
    def __init__(self):
        self.ops = []
        self.dcnt = {}
        self.res = {}

    def R(self, name):
        r = self.res.get(name)
        if r is None:
            r = self.res[name] = Res(name)
        return r

    def add(self, eng, fn, reads=(), writes=(), dkey=None):
        op = Op(eng, fn, dkey)
        deps = set()
        for r in reads:
            r = self.R(r) if isinstance(r, str) else r
            if r.w is not None:
                deps.add(r.w)
        wl = []
        for w in writes:
            w = self.R(w) if isinstance(w, str) else w
            wl.append(w)
            if w.w is not None:
                deps.add(w.w)
            deps.update(w.r.values())
        for d in deps:
            if d is op:
                continue
            if d.dkey is None and d.eng == eng and eng == "pe":
                continue
            op.deps.append(d)
            if d.dkey is None:
                d.signal = True
        if dkey is not None:
            self.dcnt[dkey] = self.dcnt.get(dkey, 0) + 16
            op.dval = self.dcnt[dkey]
        rk = eng if dkey is None else ("dma", dkey)
        for r in reads:
            r = self.R(r) if isinstance(r, str) else r
            r.r[rk] = op
        for w in wl:
            w.w = op
            w.r = {}
        self.ops.append(op)
        return op


def build_program():
    nc = bass.Bass("TRN2", target_bir_lowering=False)
    S = Sched()
    es = contextlib.ExitStack()

    def din(name, shape):
        return nc.dram_tensor(name, list(shape), F32, kind="ExternalInput").ap()

    def dout(name, shape):
        return nc.dram_tensor(name, list(shape), F32, kind="ExternalOutput").ap()

    xc = din("xc", [NROWS, D])
    spool = din("spool", [2, NS * 15, D])
    ck = din("ck", [NS, 128, 256])
    cv = din("cv", [NS, 128, 256])
    w_in_a = din("w_in_a", [2, D, 2 * D])
    w_grp_a = din("w_grp_a", [2, 4, 512, 512])
    w_out_a = din("w_out_a", [2, D, D])
    w_kv = din("w_kv", [D, 512])
    w_in_b = din("w_in_b", [2, D, 2 * D])
    w_out_b = din("w_out_b", [2, D, D])
    gains_d = din("gains", [128, 7 * NCH])
    sinks_d = din("sinks_l", [128, 2 * NCH])
    gkd_d = din("gk_dup", [128, 1])
    gqd_d = din("gq_dup", [128, 2])
    qk_d = din("qk_rows", [3, 64])
    ident_d = din("ident", [128, 128])
    dist_d = din("dist", [128, 3 * 128])
    sbias_d = din("sbias", [128, 2 * NCH])
    invc_d = din("invc", [128, 4 * 16])

    y_main = dout("y_main", [MP * NPASS, D])
    y_samp = dout("y_samp", [NS, D])
    pool_p = dout("pool_p", [2, 15, D])
    pool_s = dout("pool_s", [2, NS, 15, D])
    kwin_p = dout("kwin_p", [128, 256])
    vwin_p = dout("vwin_p", [128, 256])
    kwin_s = dout("kwin_s", [NS, 128, 256])
    vwin_s = dout("vwin_s", [NS, 128, 256])

    def sb(name, shape, dt):
        return es.enter_context(nc.sbuf_tensor(name, list(shape), dt))

    xres = sb("xres", [128, NCH, C], F32)
    h = sb("h", [128, NCH, C], BF16)
    y = sb("y", [128, NCH, C], BF16)
    wsl = [sb(f"wsl{i}", [128, 4096], BF16) for i in range(4)]
    kTp = sb("kTp", [128, 2, 4, 640], BF16)
    vtm = sb("vtm", [128, 5, 576], BF16)
    ksd = sb("ksd", [128, NS, 4, 128], BF16)
    vs = sb("vs", [128, NS, 576], BF16)
    scrf = sb("scrf", [128, 6696], F32)
    scrb = sb("scrb", [128, 8224], BF16)
    spoolT = sb("spoolT", [128, NCH, NS, 16], F32)
    unew = sb("unew", [128, 2, NCH, NS], F32)
    ustate = sb("ustate", [128, 2, NCH, 16], F32)
    rstd = sb("rstd", [128, C], F32)
    srt = rstd
    sqb = sb("sqb", [128, C], BF16)
    sqbB = sb("sqbB", [128, NS + MP], BF16)
    sqb2 = [sqb, sqbB]
    ident = sb("ident_s", [128, 128], F32)
    onesD = sb("onesD", [128, 128], BF16)
    ones128 = sb("ones128", [128, 128], BF16)
    blk64 = sb("blk64", [128, 128], BF16)
    onespad = sb("onespad", [128, 192], BF16)
    dist = sb("dist_s", [128, 3, 128], F32)
    sbias = sb("sbias_s", [128, 2, NCH], F32)
    invc = sb("invc_s", [128, 4, 16], F32)
    gains = sb("gains_s", [128, 7, NCH], F32)
    sinks = sb("sinks_s", [128, 2, NCH], F32)
    esink = sb("esink", [128, 2, NCH], F32)
    gkd = sb("gkd", [128, 1], F32)
    gq8 = sb("gq8", [128, 2], F32)
    qkb = sb("qkb", [128, 3, 64], F32)
    prod = sb("prod", [128, 64], F32)
    negM = sb("negM", [128, 2], F32)
    qspad = sb("qspad", [128, 2, 2, 4, NS], BF16)
    kout = sb("kout", [128, 256], F32)
    vout = sb("vout", [128, 256], F32)
    ktm = sb("ktm", [128, 256], F32)
    kss = sb("kss", [128, 8], F32)
    smallf = sb("smallf", [128, 64], F32)
    pTs = sb("pTs", [128, 32], BF16)
    cstage = sb("cstage", [128, NS, 256], BF16)
    vstage = sb("vstage", [128, NS, 256], BF16)
    ident_b = sb("ident_b", [128, 128], BF16)
    ps = [es.enter_context(nc.psum_tensor(f"ps{i}", [128, 512], F32)) for i in range(8)]
    _yflat = y[:].rearrange("p c n -> p (c n)")
    stgs = [_yflat[:, 0:4096].bitcast(F32), _yflat[:, 4096:8192].bitcast(F32)]
    STGR = ["stgA", "stgB", "ydup"]
    ydup = _yflat[:, 8192:8192 + NS * 512].rearrange("p (j g d e) -> p j g d e", j=NS, g=4, d=2)
    YALL = [f"y{c}_{t0}" for c in range(NCH) for t0 in (0, T1[0])]

    ubuf = [scrf[:, 0:688], scrf[:, 688:1376]]
    tmpb = [scrf[:, 1376:2064], scrf[:, 2064:2752]]
    szA = [scrf[:, 2752:3428], scrf[:, 3428:4104]]
    szB = [scrf[:, 0:2064].rearrange("p (m n) -> p m n", m=4), scrf[:, 2064:4128].rearrange("p (m n) -> p m n", m=4)]
    sbf = [scrf[:, 4128:4640], scrf[:, 4640:5152]]
    t1 = scrf[:, 5152:5664]
    qraw = [scrf[:, 5664:6180], scrf[:, 6180:6696]]
    kraw = scrf[:, 4128:4128 + C]
    krs = scrf[:, 4804:4804 + C]
    pbuf = [scrb[:, 0:2704].rearrange("p (m n) -> p m n", m=4),
            scrb[:, 2704:5408].rearrange("p (m n) -> p m n", m=4)]
    qn = [scrb[:, 0:2064].rearrange("p (m n) -> p m n", m=4), scrb[:, 2064:4128].rearrange("p (m n) -> p m n", m=4)]
    pT = [scrb[:, 4128:6176].rearrange("p (a b n) -> p a b n", a=2, b=2),
          scrb[:, 6176:8224].rearrange("p (a b n) -> p a b n", a=2, b=2)]
    SCR = []
    ksq = scrf[:, 5480:5736]
    knew = kout[0:NS, :]
    vnew = vout[0:NS, :]

    def vg(arr_ap):
        return arr_ap[:, 64:576].rearrange("p (g e) -> p g e", e=128)[:, :, 0:64]

    def wview(i, k, n):
        return wsl[i][:, 0:k * n].rearrange("p (k n) -> p k n", k=k)

    wfifo = [0, 1, 2, 3]

    def unpin(i):
        wfifo.append(i)

    def load_panel(src_aps, dst_fn, pinned=False):
        i = wfifo.pop(0)
        if not pinned:
            wfifo.append(i)
        for dv, sa in zip(dst_fn, src_aps):
            def fn(e, dv=dv, sa=sa, i=i):
                return e.dma_start(out=dv(i), in_=sa)
            S.add("pool", fn, reads=["xloaded"], writes=[f"wsl{i}"], dkey=f"wsl{i}")
        return i

    def load_std(wap, col0, ncol=256):
        src = wap.rearrange("(k p) n -> p k n", p=128)[:, :, col0:col0 + ncol]
        return load_panel([src], [lambda i: wview(i, 16, ncol)])

    def dma_sp(out, in_, reads=(), writes=(), dkey="par"):
        def fn(e):
            return e.dma_start(out=out, in_=in_)
        return S.add("sp", fn, reads=reads, writes=writes, dkey=dkey)

    P = "params"
    dma_sp(gains[:].rearrange("p a c -> p (a c)"), gains_d, writes=[P])
    dma_sp(sinks[:].rearrange("p a c -> p (a c)"), sinks_d, writes=[P])
    dma_sp(gkd[:], gkd_d, writes=[P])
    dma_sp(gq8[:], gqd_d, writes=[P])
    dma_sp(qkb[:].rearrange("p a c -> p (a c)"), qk_d.rearrange("a c -> (a c)").partition_broadcast(128), writes=[P])
    dma_sp(ident[:], ident_d, writes=[P])
    dma_sp(dist[:].rearrange("p a c -> p (a c)"), dist_d, writes=[P])
    dma_sp(sbias[:].rearrange("p a c -> p (a c)"), sbias_d, writes=[P])
    dma_sp(invc[:].rearrange("p a c -> p (a c)"), invc_d, writes=[P])

    def vec(fn, reads=(), writes=()):
        return S.add("dve", fn, reads=reads, writes=writes)

    def act(fn, reads=(), writes=()):
        return S.add("act", fn, reads=reads, writes=writes)

    def pe(fn, reads=(), writes=()):
        return S.add("pe", fn, reads=reads, writes=writes)

    def init_consts(e):
        e.memset(onesD[:], 1.0 / D)
        e.memset(ones128[:], 1.0 / 128)
        e.memset(blk64[:], 0.0)
        return e.memset(onespad[:], 0.0)

    def init_consts2(e):
        e.memset(blk64[0:64, 0:64], 1.0 / 64)
        e.memset(blk64[64:128, 64:128], 1.0 / 64)
        return e.memset(onespad[:, 64:128], 1.0)

    def init_zeros(e):
        ins = None
        for t_ in (scrf, scrb, kTp, vtm, vs, ksd, qspad, ustate, spoolT):
            ins = e.memzero(t_[:])
        return ins

    def init_cst(e):
        e.memzero(cstage[:])
        return e.memzero(vstage[:])
    act(init_cst, writes=[f"cst{j}" for j in range(NS)] + [f"vst{j}" for j in range(NS)])
    vec(init_consts, writes=["consts"])
    vec(lambda e: e.tensor_copy(out=ident_b[:], in_=ident[:]), reads=[P], writes=["ident_b"])
    vec(init_consts2, reads=["consts"], writes=["consts"])
    act(init_zeros, writes=["kTp", "vtm", "vs", "ksd", "qspad", "ubuf0", "ubuf1", "tmp0", "tmp1", "szA0", "szA1",
                            "pbuf0", "pbuf1", "ustate", "spoolT", "kraw", "qspad0", "qspad1"])

    for j in range(2):
        for fn in (lambda e, j=j: e.tensor_tensor(out=prod[:], in0=qkb[:, j, :], in1=qkb[:, 2, :], op=ALU.mult),
                   lambda e, j=j: e.tensor_reduce(out=negM[:, j:j + 1], in_=prod[:], axis=AX.X, op=ALU.max, apply_absolute_value=True),
                   lambda e, j=j: e.tensor_scalar(out=negM[:, j:j + 1], in0=negM[:, j:j + 1], scalar1=-8.0, scalar2=None, op0=ALU.mult)):
            vec(fn, reads=[P, "negM"], writes=["negM"])

        def fn2(e, j=j):
            return e.activation(out=esink[:, j, :], in_=sinks[:, j, :], func=AF.Exp, bias=negM[:, j:j + 1], scale=1.0)
        act(fn2, reads=[P, "negM"], writes=["esink"])
    vec(lambda e: e.tensor_scalar(out=gq8[:], in0=gq8[:], scalar1=0.125, scalar2=None, op0=ALU.mult),
        reads=[P], writes=["gq8"])

    bank_rr = {"i": 0}

    def next_pair():
        i = bank_rr["i"] % 3
        bank_rr["i"] += 1
        return 2 * i, 2 * i + 1

    def hres(c, t):
        return f"h{c}_{t[0]}"

    def xr(c, t):
        return f"x{c}_{t[0]}"

    def yres(c, t):
        return f"y{c}_{t[0]}"

    def pv(banks, t, a=None, b=None):
        bk, c0 = banks[t]
        n = t[1] - t[0]
        a = 0 if a is None else a
        b = n if b is None else b
        return ps[bk][:, c0 + a:c0 + b]

    def pr(banks, tiles):
        return [f"ps{banks[t][0]}" for t in tiles]

    def chunk_matmul(slot, wk, wcol, rhs_arr, rhs_res_fn, tiles, nk=NCH, kview=None, extra_reads=(), dest=None, ksplit=None):
        if dest is None:
            b0, b1 = next_pair()
            banks = {}
            for t in tiles:
                banks[t] = (b0 if (t[1] - t[0]) < 512 else b1, 0)
            if len(tiles) == 2 and banks[tiles[0]][0] == banks[tiles[1]][0]:
                banks[tiles[1]] = (b1, 0)
        else:
            banks = dest
        wv = kview if kview is not None else wview(slot, wk, 256)

        ks = ksplit if ksplit else nk
        for k0 in range(0, nk, ks):
            def fn(e, k0=k0):
                ins = None
                for k in range(k0, min(nk, k0 + ks)):
                    for t in tiles:
                        ins = e.matmul(pv(banks, t), lhsT=wv[:, k, wcol:wcol + 128],
                                       rhs=rhs_arr[:, k, t[0]:t[1]], start=(k == 0), stop=(k == nk - 1))
                return ins
            reads = [f"wsl{slot}"] + [rhs_res_fn(k, t) for k in range(k0, min(nk, k0 + ks)) for t in tiles] + list(extra_reads)
            pe(fn, reads=reads, writes=pr(banks, tiles))
        return banks

    def stat_dest(t):
        return (6, 0) if (t[1] - t[0]) == 512 else (7, 0)

    def emit_square(c, t):
        act(lambda e: e.activation(out=h[:, c, t[0]:t[1]], in_=xres[:, c, t[0]:t[1]], func=AF.Square),
            reads=[xr(c, t)], writes=[hres(c, t)])

    def emit_stat(c, t):
        bk, c0 = stat_dest(t)
        n = t[1] - t[0]
        pe(lambda e: e.matmul(ps[bk][:, c0:c0 + n], lhsT=onesD[:], rhs=h[:, c, t[0]:t[1]], start=(c == 0), stop=(c == NCH - 1)),
           reads=["consts", hres(c, t)], writes=[f"ps{bk}"])

    def rmsnorm(gi, tiles, mode="full"):
        if mode == "full":
            for t in tiles:
                for c in range(NCH):
                    emit_square(c, t)
                for c in range(NCH):
                    emit_stat(c, t)
        for t in tiles:
            n = t[1] - t[0]
            if mode != "reuse":
                bk, c0 = stat_dest(t)
                chain("act", [lambda e, t=t, n=n, bk=bk, c0=c0: e.activation(out=rstd[:, t[0]:t[1]], in_=ps[bk][:, c0:c0 + n], func=AF.Ln,
                                                                             bias=EPS, scale=1.0),
                              lambda e, t=t: e.activation(out=rstd[:, t[0]:t[1]], in_=rstd[:, t[0]:t[1]], func=AF.Exp, scale=-0.5)],
                      reads=[f"ps{bk}"], writes=["rstd"])
        for c in range(NCH):
            for t in tiles:
                def fh(e, c=c, t=t):
                    return e.scalar_tensor_tensor(out=h[:, c, t[0]:t[1]], in0=xres[:, c, t[0]:t[1]],
                                                  scalar=gains[:, gi, c:c + 1], in1=rstd[:, t[0]:t[1]],
                                                  op0=ALU.mult, op1=ALU.mult)
                vec(fh, reads=[xr(c, t), "rstd", P], writes=[hres(c, t)])

    def residual_add(banks, c, tiles):
        for t in tiles:
            n = t[1] - t[0]

            def fn(e, t=t, n=n):
                return e.tensor_tensor(out=xres[:, c, t[0]:t[1]], in0=pv(banks, t), in1=xres[:, c, t[0]:t[1]], op=ALU.add)
            vec(fn, reads=pr(banks, [t]) + [xr(c, t)], writes=[xr(c, t)])

    def w_out_phase(wap, tiles, norm_tiles=None, hook=None):
        pending = []
        for jj in range(8):
            slot = load_std(wap, jj * 256)
            if jj == 2 and hook is not None:
                hook()
            for mm in range(2):
                c = 2 * jj + mm
                banks = chunk_matmul(slot, 16, mm * 128, y, yres, tiles)
                residual_add(banks, c, tiles)
                if norm_tiles:
                    for t in norm_tiles:
                        emit_square(c, t)
                    pending.append(c)
                    if len(pending) > 2:
                        c2 = pending.pop(0)
                        for t in norm_tiles:
                            emit_stat(c2, t)
        for c2 in pending:
            for t in norm_tiles:
                emit_stat(c2, t)

    def xall(tiles):
        return [xr(c, t) for c in range(NCH) for t in tiles]

    def hall(tiles):
        return [hres(c, t) for c in range(NCH) for t in tiles]

    def yall(tiles):
        return [yres(c, t) for c in range(NCH) for t in tiles]

    ALLT = [T0, T1, TS]

    def load_x(p):
        if p == 0:
            row_tiles = [(r, min(r + 128, C)) for r in range(0, C, 128)]
            coff = 0
        else:
            row_tiles = [(C + r, C + r + 128) for r in range(0, MP, 128)]
            coff = T1[0] - C
        for ti, (r0, r1) in enumerate(row_tiles):
            nr = r1 - r0
            st = stgs[ti % 2]
            sr = STGR[ti % 2]
            dma_sp(st[0:nr, :], xc[r0:r1, :], writes=[sr] + (YALL if ti == 0 else []) + (["xloaded"] if (p == 0 and ti == 1) else []),
                   dkey=sr)
            for cq in range(4):
                bk = 6 + (cq % 2)

                def fp(e, cq=cq, bk=bk, st=st):
                    ins = None
                    for i in range(4):
                        c = 4 * cq + i
                        ins = e.transpose(ps[bk][:, i * 128:(i + 1) * 128], st[:, c * 128:(c + 1) * 128], ident[:])
                    return ins
                pe(fp, reads=[sr, P], writes=[f"ps{bk}"])
                c0 = r0 + coff

                def fv(e, cq=cq, bk=bk, nr=nr, c0=c0):
                    return e.tensor_copy(out=xres[:, 4 * cq:4 * cq + 4, c0:c0 + nr],
                                         in_=ps[bk][:].rearrange("p (c n) -> p c n", c=4)[:, :, 0:nr])
                vec(fv, reads=[f"ps{bk}"], writes=[xr(c, t) for c in range(4 * cq, 4 * cq + 4) for t in ALLT])

    spstg = scrf[:, 0:2048]
    SPG = ["ubuf0", "ubuf1", "tmp0"]

    def load_spool(l):
        dma_sp(spstg[0:NS * 15, :], spool[l], writes=["spstg"] + SPG, dkey="spstg")
        for cq in range(4):
            _, bk = next_pair()

            def fp(e, cq=cq, bk=bk):
                ins = None
                for i in range(4):
                    c = 4 * cq + i
                    ins = e.transpose(ps[bk][:, i * 128:(i + 1) * 128], spstg[:, c * 128:(c + 1) * 128], ident[:])
                return ins
            pe(fp, reads=["spstg", P], writes=[f"ps{bk}"])

            def fv(e, cq=cq, bk=bk):
                return e.tensor_copy(out=spoolT[:, 4 * cq:4 * cq + 4, :, 0:15],
                                     in_=ps[bk][:].rearrange("p (c n) -> p c n", c=4)[:, :, 0:60].rearrange("p c (j r) -> p c j r", j=NS))
            vec(fv, reads=[f"ps{bk}"], writes=["spoolT"])

        def fzero(e):
            e.memset(ubuf[0][:, 0:16], 0.0)
            return e.memset(ubuf[1][:, 0:16], 0.0)
        vec(fzero, reads=["spstg"], writes=["spstg"] + SPG)

    tokc = {"i": 0}

    def chain(eng, fns, reads=(), writes=()):
        tokc["i"] += 1
        tok = f"_tok{tokc['i']}"
        op = None
        for i, fn in enumerate(fns):
            rd = list(reads) + ([tok] if i > 0 else [])
            op = S.add(eng, fn, reads=rd, writes=list(writes) + [tok])
        return op

    def a_group(p, l, g, tiles):
        wa = w_in_a[l]
        w = 2 << g
        pb = pbuf[g % 2]
        pres = f"pbuf{g % 2}"
        for hp in range(2):
            slot = load_std(wa, g * 512 + hp * 256)
            for mm in range(2):
                m = 2 * hp + mm
                c = 4 * g + m
                banks = chunk_matmul(slot, 16, mm * 128, h, hres, tiles, ksplit=(2 if (g == 0 and hp == 0) else None))
                ub = ubuf[c % 2]
                ubr = f"ubuf{c % 2}"
                if p == 1:
                    vec(lambda e, ub=ub, c=c: e.tensor_copy(out=ub[:, 160:176], in_=ustate[:, l, c, :]),
                        reads=["ustate"], writes=[ubr])

                def fe(e, ub=ub, banks=banks, c=c):
                    ins = e.activation(out=ub[:, 176:688], in_=pv(banks, T1), func=AF.Copy)
                    if p == 0:
                        ins = e.activation(out=ub[:, 16:176], in_=pv(banks, T0, NS, NS + HALO), func=AF.Copy)
                        ins = e.activation(out=spoolT[:, c, :, 15], in_=pv(banks, T0, 0, NS), func=AF.Copy)
                        ins = e.activation(out=unew[:, l, c, :], in_=pv(banks, T0, 0, NS), func=AF.Copy)
                    return ins
                act(fe, reads=pr(banks, tiles), writes=[ubr, "spoolT", "unew"])
                vec(lambda e, ub=ub, c=c: e.tensor_copy(out=ustate[:, l, c, :], in_=ub[:, 672:688]),
                    reads=[ubr], writes=["ustate"])
                cur = ub
                curr = ubr
                step = 1
                ti = 0
                while step < w:
                    dst = tmpb[ti % 2]
                    dstr = f"tmp{ti % 2}"
                    lo = 2 * step - 1 + (160 if p == 1 else 0)

                    def fs(e, cur=cur, dst=dst, lo=lo, step=step):
                        return e.tensor_tensor(out=dst[:, lo:688], in0=cur[:, lo:688], in1=cur[:, lo - step:688 - step], op=ALU.add)
                    vec(fs, reads=[curr], writes=[dstr])
                    cur, curr = dst, dstr
                    step *= 2
                    ti += 1
                c0 = NS if p == 0 else T1[0]
                u0 = c0 + 12
                fns = [lambda e, cur=cur, ub=ub, m=m: e.scalar_tensor_tensor(
                    out=pb[:, m, c0:C], in0=cur[:, u0:688], scalar=1.0 / w, in1=ub[:, u0:688], op0=ALU.mult, op1=ALU.subtract)]
                if p == 0:
                    fns.append(lambda e, cur=cur: e.tensor_tensor(out=smallf[:, 0:16], in0=cur[:, 176:192], in1=invc[:, g, :], op=ALU.mult))
                    fns.append(lambda e, ub=ub, m=m: e.tensor_tensor(out=pb[:, m, T1[0]:T1[0] + 16], in0=smallf[:, 0:16],
                                                                      in1=ub[:, 176:192], op=ALU.subtract))
                    fns.append(lambda e, c=c: e.tensor_reduce(out=smallf[:, 16:16 + NS], in_=spoolT[:, c, :, 16 - w:16], axis=AX.X, op=ALU.add))
                    fns.append(lambda e, c=c, m=m: e.scalar_tensor_tensor(out=pb[:, m, 0:NS], in0=smallf[:, 16:16 + NS], scalar=1.0 / w,
                                                                         in1=spoolT[:, c, :, 15], op0=ALU.mult, op1=ALU.subtract))
                chain("dve", fns, reads=[curr, ubr, P, "spoolT"], writes=[pres, "smallf"])
        gsl = load_panel([w_grp_a[l, g].rearrange("(k p) n -> p k n", p=128)], [lambda i: wview(i, 4, 512)], pinned=True)
        gview = wview(gsl, 4, 512)
        for hp in range(2):
            slot = load_std(wa, D + g * 512 + hp * 256)
            zbs = []
            for mm in range(2):
                m = 2 * hp + mm
                c = 4 * g + m
                zb = chunk_matmul(slot, 16, mm * 128, h, hres, tiles)
                sz = szA[c % 2]
                szr = f"szA{c % 2}"

                def fz(e, zb=zb, sz=sz):
                    ins = None
                    for t in tiles:
                        ins = e.activation(out=sz[:, t[0]:t[1]], in_=pv(zb, t), func=AF.Silu)
                    return ins
                act(fz, reads=pr(zb, tiles), writes=[szr])
            for mm in range(2):
                m = 2 * hp + mm
                c = 4 * g + m
                sz = szA[c % 2]
                szr = f"szA{c % 2}"
                gb = chunk_matmul(gsl, 4, m * 128, pb, lambda k, t: pres, tiles, nk=4, kview=gview)

                def fy(e, gb=gb, sz=sz, c=c):
                    ins = None
                    for t in tiles:
                        ins = e.scalar_tensor_tensor(out=y[:, c, t[0]:t[1]], in0=pv(gb, t),
                                                     scalar=gains[:, 5 + l, c:c + 1], in1=sz[:, t[0]:t[1]],
                                                     op0=ALU.mult, op1=ALU.mult)
                    return ins
                vec(fy, reads=pr(gb, tiles) + [szr, P], writes=[yres(c, t) for t in tiles])
        unpin(gsl)

    def a_layer(p, l):
        tiles = [T0, T1] if p == 0 else [T1]
        rmsnorm(l, tiles, mode=("full" if l == 0 else "stats_ready"))
        if p == 0:
            for j in range(NS):
                dma_sp(pool_s[l, j, 0:14, :], spool[l, j * 15 + 1:j * 15 + 15, :], dkey="misc")
        for g in range(4):
            a_group(p, l, g, tiles)
        w_out_phase(w_out_a[l], tiles, norm_tiles=tiles,
                    hook=((lambda: load_spool(1)) if (p == 0 and l == 0) else None))

    def kv_kfm(p, g, slot, gi, tiles):
        kb = chunk_matmul(slot, 16, gi * 128, h, hres, tiles, ksplit=(2 if g == 0 else None))

        def fk(e):
            ins = None
            for t in tiles:
                n = t[1] - t[0]
                e.activation(out=kraw[:, t[0]:t[1]], in_=pv(kb, t), func=AF.Copy)
                ins = e.activation(out=sqb[:, t[0]:t[1]], in_=pv(kb, t), func=AF.Square)
            return ins
        act(fk, reads=pr(kb, tiles), writes=["kraw", "sqb"])
        for t in tiles:
            n = t[1] - t[0]
            pe(lambda e, t=t, n=n: e.matmul(ps[6][:, 0:n], lhsT=ones128[:], rhs=sqb[:, t[0]:t[1]], start=True, stop=True),
               reads=["sqb", "consts"], writes=["ps6"])
            chain("act", [lambda e, t=t, n=n: e.activation(out=krs[:, t[0]:t[1]], in_=ps[6][:, 0:n], func=AF.Ln, bias=EPS, scale=1.0),
                          lambda e, t=t: e.activation(out=krs[:, t[0]:t[1]], in_=krs[:, t[0]:t[1]], func=AF.Exp, scale=-0.5)],
                  reads=["ps6"], writes=["krs"])
        lo = tiles[0][0]
        fns = [lambda e: e.scalar_tensor_tensor(out=kraw[:, lo:C], in0=kraw[:, lo:C], scalar=gkd[:, 0:1], in1=krs[:, lo:C],
                                                op0=ALU.mult, op1=ALU.mult)]

        def fcp(e):
            e.tensor_copy(out=kTp[0:64, 0, g, 128:640], in_=kraw[0:64, T1[0]:T1[1]])
            ins = e.tensor_copy(out=kTp[64:128, 1, g, 128:640], in_=kraw[64:128, T1[0]:T1[1]])
            if p == 0:
                e.tensor_copy(out=kTp[0:64, 0, g, 0:128], in_=kraw[0:64, T0[1] - 128:T0[1]])
                e.tensor_copy(out=kTp[64:128, 1, g, 0:128], in_=kraw[64:128, T0[1] - 128:T0[1]])
                ins = e.tensor_copy(out=ksd[:, :, g, 127], in_=kraw[:, 0:NS])
            return ins
        fns.append(fcp)
        chain("dve", fns, reads=["krs", "kraw", P], writes=["kraw", "kTp", "ksd"])

    def tm_k(rows, c0, dst, dres, slk, kv_k, tiles):
        def fk(e):
            ins = None
            for k in range(NCH):
                ins = e.matmul(ps[7][0:rows, 256:512], lhsT=h[:, k, c0:c0 + rows], rhs=kv_k[:, k, :], start=(k == 0), stop=(k == NCH - 1))
            return ins
        pe(fk, reads=[f"wsl{slk}"] + hall(tiles), writes=["ps7"])
        g4 = lambda ap: ap.rearrange("p (g e) -> p g e", g=4)
        chain("dve", [
            lambda e: e.tensor_copy(out=ktm[0:rows, :], in_=ps[7][0:rows, 256:512]),
            lambda e: e.tensor_tensor(out=ksq[0:rows, :], in0=ktm[0:rows, :], in1=ktm[0:rows, :], op=ALU.mult),
            lambda e: e.tensor_reduce(out=kss[0:rows, 0:4], in_=g4(ksq[0:rows, :]), axis=AX.X, op=ALU.add),
        ], reads=["ps7"], writes=["ktm", "kss", "ksq"])
        act(lambda e: e.activation(out=kss[0:rows, 4:8], in_=kss[0:rows, 0:4], func=AF.Sqrt, bias=EPS, scale=1.0 / 64),
            reads=["kss"], writes=["kss2"])
        chain("dve", [
            lambda e: e.reciprocal(out=kss[0:rows, 0:4], in_=kss[0:rows, 4:8]),
            lambda e: e.tensor_tensor(out=g4(ktm[0:rows, :]), in0=g4(ktm[0:rows, :]),
                                      in1=kss[0:rows, 0:4].unsqueeze(2).broadcast_to([rows, 4, 64]), op=ALU.mult),
            lambda e: e.tensor_tensor(out=g4(dst), in0=g4(ktm[0:rows, :]),
                                      in1=qkb[0:rows, 2, :].unsqueeze(1).broadcast_to([rows, 4, 64]), op=ALU.mult),
        ], reads=["kss2", "ktm", P], writes=["kss", "ktm", dres])

    def kv_phase(p):
        tiles = [T0, T1] if p == 0 else [T1]
        rmsnorm(2, tiles, mode="stats_ready")
        wkv3 = w_kv.rearrange("(k p) n -> p k n", p=128)
        if p == 1:
            chain("dve", [lambda e: e.tensor_copy(out=kTp[:, :, :, 0:128], in_=kTp[:, :, :, 512:640]),
                          lambda e: e.tensor_copy(out=vtm[:, 0, :], in_=vtm[:, 4, :])],
                  reads=["kTp", "vtm"], writes=["kTp", "vtm"])
        for gh in range(2):
            srcs, dsts = [], []
            for d in range(2):
                for gi in range(2):
                    srcs.append(wkv3[:, :, gh * 128 + gi * 64:gh * 128 + gi * 64 + 64])
                    dsts.append(lambda i, d=d, gi=gi: wsl[i][:].rearrange("p (k g d e) -> p k g d e", k=16, g=2, d=2)[:, :, gi, d, :])
            slot = load_panel(srcs, dsts)
            for gi in range(2):
                kv_kfm(p, 2 * gh + gi, slot, gi, tiles)
        if p == 0:
            transpose_caches()
        slk = load_std(w_kv, 0)
        slv = load_std(w_kv, 256)
        kv_k = wview(slk, 16, 256)
        kv_v = wview(slv, 16, 256)
        blocks = []
        if p == 0:
            blocks.append((T0[1] - 128, 128, 0))
        for bi in range(4):
            blocks.append((T1[0] + bi * 128, 128, bi + 1))
        for vi, (c0, ncol, blk) in enumerate(blocks):
            vb = 7 if vi % 2 == 0 else 6

            def fv(e, c0=c0, ncol=ncol, vb=vb):
                ins = None
                for k in range(NCH):
                    ins = e.matmul(ps[vb][0:ncol, 0:256], lhsT=h[:, k, c0:c0 + ncol], rhs=kv_v[:, k, :], start=(k == 0), stop=(k == NCH - 1))
                return ins
            pe(fv, reads=[f"wsl{slv}"] + hall(tiles), writes=[f"ps{vb}"])
            last = (p == 1 and blk == 4)

            def fe(e, blk=blk, last=last, vb=vb):
                ins = e.tensor_copy(out=vg(vtm[:, blk, :]), in_=ps[vb][:, 0:256].rearrange("p (g e) -> p g e", g=4))
                if last:
                    ins = e.tensor_copy(out=vout[:], in_=ps[vb][:, 0:256])
                return ins
            vec(fe, reads=[f"ps{vb}"], writes=["vtm", "vout"])
            if last:
                outs.append(dma_sp(vwin_p, vout[:], reads=["vout"], dkey="vout"))
        if p == 1:
            tm_k(128, T1[1] - 128, kout[:], "kout", slk, kv_k, tiles)
            outs.append(dma_sp(kwin_p, kout[:], reads=["kout"], dkey="kout"))
        else:
            tm_k(NS, 0, knew, "kout", slk, kv_k, tiles)

            def fvs(e):
                ins = None
                for k in range(NCH):
                    ins = e.matmul(ps[7][0:NS, 0:256], lhsT=h[:, k, 0:NS], rhs=kv_v[:, k, :], start=(k == 0), stop=(k == NCH - 1))
                return ins
            pe(fvs, reads=[f"wsl{slv}"] + hall(tiles), writes=["ps7"])
            vec(lambda e: e.tensor_copy(out=vnew, in_=ps[7][0:NS, 0:256]), reads=["ps7"], writes=["vout"])
            for j in range(NS):
                outs.append(dma_sp(kwin_s[j, 127:128, :], knew[j:j + 1, :], reads=["kout"], dkey="kout"))
                outs.append(dma_sp(vwin_s[j, 127:128, :], vnew[j:j + 1, :], reads=["vout"], dkey="vout"))

                def fd(e, j=j):
                    return e.dma_start(out=vg(vs[127:128, j, :]), in_=vnew[j:j + 1, :].rearrange("p (g e) -> p g e", g=4))
                S.add("pool", fd, reads=["vout", "vs"], writes=["vs"], dkey="vs")

    def load_caches():
        for j in range(NS):
            S.add("pool", lambda e, j=j: e.dma_start(out=cstage[0:127, j, :], in_=ck[j, 1:128, :]),
                  writes=[f"cst{j}"], dkey=f"cst{j}")
            S.add("pool", lambda e, j=j: e.dma_start(out=vstage[0:127, j, :], in_=cv[j, 1:128, :]),
                  writes=[f"vst{j}"], dkey=f"vst{j}")
            dma_sp(kwin_s[j, 0:127, :], ck[j, 1:128, :], dkey="misc")
            dma_sp(vwin_s[j, 0:127, :], cv[j, 1:128, :], dkey="misc")

    def transpose_caches():
        psb = ps[7][:].bitcast(BF16)
        for j in range(NS):
            vec(lambda e, j=j: e.tensor_copy(out=vg(vs[:, j, :]), in_=vstage[:, j, :].rearrange("p (g e) -> p g e", g=4)),
                reads=[f"vst{j}"], writes=["vs"])
        for d in range(2):
            vec(lambda e, d=d: e.tensor_copy(out=ydup[:, :, :, d, :], in_=cstage[:].rearrange("p j (g e) -> p j g e", e=64)),
                reads=[f"cst{j}" for j in range(NS)], writes=["ydup"] + (YALL if d == 0 else []))
        for j in range(NS):
            def fp(e, j=j):
                ins = None
                for g in range(4):
                    ins = e.transpose(psb[:, g * 128:(g + 1) * 128], ydup[:, j, g].rearrange("p d e -> p (d e)"), ident_b[:])
                return ins
            pe(fp, reads=["ydup", "ident_b"], writes=["ps7"])
            vec(lambda e, j=j: e.tensor_copy(out=ksd[:, j, :, 0:127], in_=psb[:, 0:512].rearrange("p (g n) -> p g n", g=4)[:, :, 0:127]),
                reads=["ps7"], writes=["ksd"])

    def slope(hd):
        return float(2.0 ** (-(hd + 1) / 4.0))

    def bcol(t):
        return (0, NS) if t == TS else (NS, NS + MP)

    bch = {"i": 0}
    sring = {"i": 0}
    OB, DB = 3, 4

    def b_dest(tiles, alt=False):
        i = bch["i"] % 2
        bch["i"] += 1
        d = {}
        for t in tiles:
            d[t] = (7, 500) if t == TS else ((1 if (alt and i) else 5), 0)
        return d

    def b_q_steps(p, j, g, tiles):
        wb = w_in_b[j]
        gb_ = g % 2
        st = {}

        def step_a(m):
            hp, mm = m // 2, m % 2
            if mm == 0:
                st["slot"] = load_std(wb, g * 512 + hp * 256)
            slot = st["slot"]
            c = 4 * g + m
            qb = chunk_matmul(slot, 16, mm * 128, h, hres, tiles, dest=b_dest(tiles, alt=(g == 0)), ksplit=(2 if (g == 0 and m == 0) else None))
            qr = qraw[c % 2]
            qrr = f"qraw{c % 2}"
            sq_ = sqb2[c % 2]
            sqr = f"sqb{c % 2}"

            def fq(e):
                ins = None
                for t in tiles:
                    b = bcol(t)
                    e.activation(out=qr[:, b[0]:b[1]], in_=pv(qb, t), func=AF.Copy)
                    ins = e.activation(out=sq_[:, b[0]:b[1]], in_=pv(qb, t), func=AF.Square)
                return ins
            act(fq, reads=pr(qb, tiles), writes=[qrr, sqr])

        def step_b(m):
            c = 4 * g + m
            qr = qraw[c % 2]
            qrr = f"qraw{c % 2}"
            sq_ = sqb2[c % 2]
            sqr = f"sqb{c % 2}"
            for t in tiles:
                b = bcol(t)
                n = t[1] - t[0]
                sbk = 7 if p == 1 else 6
                pe(lambda e, b=b, n=n, sbk=sbk: e.matmul(ps[sbk][:, 0:n], lhsT=blk64[:], rhs=sq_[:, b[0]:b[1]], start=True, stop=True),
                   reads=[sqr, "consts"], writes=[f"ps{sbk}"])
                act(lambda e, b=b, n=n, sbk=sbk: e.activation(out=rstd[:, b[0]:b[1]], in_=ps[sbk][:, 0:n], func=AF.Ln, bias=EPS, scale=1.0),
                    reads=[f"ps{sbk}"], writes=["rstd"])
            lo = bcol(tiles[0])[0]
            hi = NS + MP
            act(lambda e: e.activation(out=rstd[:, lo:hi], in_=rstd[:, lo:hi], func=AF.Exp, scale=-0.5),
                reads=["rstd"], writes=["rstd"])
            fns = [lambda e: e.scalar_tensor_tensor(out=qn[gb_][:, m, lo:hi], in0=qr[:, lo:hi], scalar=gq8[:, j:j + 1],
                                                    in1=rstd[:, lo:hi], op0=ALU.mult, op1=ALU.mult)]
            if p == 0:
                def fqs(e):
                    e.tensor_copy(out=qspad[0:64, gb_, 0, m, :], in_=qn[gb_][0:64, m, 0:NS])
                    return e.tensor_copy(out=qspad[64:128, gb_, 1, m, :], in_=qn[gb_][64:128, m, 0:NS])
                fns.append(fqs)
            chain("dve", fns, reads=["rstd", qrr, "gq8"], writes=[f"qn{gb_}", f"qspad{gb_}"])
        return [lambda m=m: step_a(m) for m in range(4)], [lambda m=m: step_b(m) for m in range(4)]

    def b_z_steps(p, j, g, tiles):
        wb = w_in_b[j]
        gb_ = g % 2
        st = {}

        def step(m):
            hp, mm = m // 2, m % 2
            if mm == 0:
                st["slot"] = load_std(wb, D + g * 512 + hp * 256)
            zd = b_dest(tiles, alt=(g == 0))
            if g > 0 and m == 2:
                zd[T1] = (1, 0)
            zb = chunk_matmul(st["slot"], 16, mm * 128, h, hres, tiles, dest=zd)

            def fz(e):
                ins = None
                for t in tiles:
                    b = bcol(t)
                    ins = e.activation(out=szB[gb_][:, m, b[0]:b[1]], in_=pv(zb, t), func=AF.Silu)
                return ins
            act(fz, reads=pr(zb, tiles), writes=[f"szB{gb_}"])
        return [lambda m=m: step(m) for m in range(4)]

    def b_att_steps(p, j, g):
        gb_ = g % 2
        qng, szg = qn[gb_], szB[gb_]
        qnr, szr = f"qn{gb_}", f"szB{gb_}"

        def emit_S(bi):
            q0 = NS + (bi - 1) * 128
            pt = pT[bi % 2]
            ptr = f"pT{bi % 2}"
            for kbi, kb in enumerate((bi - 1, bi)):
                if kbi == 1:
                    dsel = 0
                else:
                    dsel = 2 if (p == 0 and bi == 1) else 1
                for par in range(2):
                    sbl = [0, 1, 2, 6] if p == 1 else [0, 1, 2]
                    bk = sbl[sring["i"] % len(sbl)]
                    sring["i"] += 1
                    pe(lambda e, bk=bk, par=par, kb=kb: e.matmul(
                        ps[bk][:, :].rearrange("p (m n) -> p m n", m=4), lhsT=kTp[:, par, g, kb * 128:(kb + 1) * 128],
                        rhs=qng[:, :, q0:q0 + 128], start=True, stop=True),
                       reads=["kTp", qnr], writes=[f"ps{bk}"])
                    sbt = sbf[sring["i"] % 2]
                    sbr = f"sbf{sring['i'] % 2}"

                    def fb(e, bk=bk, par=par, dsel=dsel, sbt=sbt):
                        ins = None
                        for m in range(4):
                            hd = 2 * (4 * g + m) + par
                            ins = e.scalar_tensor_tensor(out=sbt[:, m * 128:(m + 1) * 128], in0=dist[:, dsel, :], scalar=-slope(hd),
                                                         in1=ps[bk][:, m * 128:(m + 1) * 128], op0=ALU.mult, op1=ALU.add)
                        return ins
                    vec(fb, reads=[f"ps{bk}", P], writes=[sbr])
                    act(lambda e, sbt=sbt, kbi=kbi, par=par: e.activation(out=pt[:, kbi, par, :], in_=sbt[:], func=AF.Exp,
                                                                          bias=negM[:, j:j + 1], scale=1.0),
                        reads=[sbr, "negM"], writes=[ptr])

        def emit_PV(bi):
            q0 = NS + (bi - 1) * 128
            hc0 = T1[0] + (bi - 1) * 128
            pt = pT[bi % 2]
            ptr = f"pT{bi % 2}"

            def fo(e):
                ins = None
                for (ob, use_v) in ((OB, True), (DB, False)):
                    i = 0
                    for kbi, kb in enumerate((bi - 1, bi)):
                        for par in range(2):
                            if use_v:
                                o = (64 if par == 0 else 0) + 128 * g
                                lt = vtm[:, kb, o:o + 128]
                            else:
                                lt = onespad[:, 64:192] if par == 0 else onespad[:, 0:128]
                            ins = e.matmul(ps[ob][:, :], lhsT=lt, rhs=pt[:, kbi, par, :], start=(i == 0), stop=(i == 3))
                            i += 1
                return ins
            pe(fo, reads=[ptr, "vtm", "consts"], writes=[f"ps{OB}", f"ps{DB}"])

            def f4a(e):
                ins = None
                for m in range(4):
                    ins = e.activation(out=t1[:, m * 128:(m + 1) * 128], in_=ps[DB][:, m * 128:(m + 1) * 128], func=AF.Ln,
                                       bias=esink[:, j, 4 * g + m:4 * g + m + 1], scale=1.0)
                return ins
            chain("act", [f4a, lambda e: e.activation(out=t1[:], in_=t1[:], func=AF.Exp, scale=-1.0)],
                  reads=[f"ps{DB}", "esink"], writes=["t1"])
            chain("dve", [
                lambda e: e.tensor_tensor(out=t1[:], in0=ps[OB][:, :], in1=t1[:], op=ALU.mult),
                lambda e: e.tensor_tensor(out=y[:, 4 * g:4 * g + 4, hc0:hc0 + 128], in0=t1[:].rearrange("p (m n) -> p m n", m=4),
                                          in1=szg[:, :, q0:q0 + 128], op=ALU.mult),
            ], reads=[f"ps{OB}", "t1", szr], writes=["t1"] + [yres(c, T1) for c in range(4 * g, 4 * g + 4)])

        def emit_samples(part):
            if part == 1:
                return emit_samples_b()

            def fss(e):
                ins = None
                for js in range(NS):
                    for par in range(2):
                        o = (js * 2 + par) * 4
                        ins = e.matmul(ps[7][:, o:o + 4], lhsT=ksd[:, js, g, :], rhs=qspad[:, gb_, par, :, js], start=True, stop=True)
                return ins
            pe(fss, reads=["ksd", f"qspad{gb_}"], writes=["ps7"])

            def fsb(e):
                return e.tensor_tensor(out=smallf[:, 0:32].rearrange("p (s a m) -> p s a m", s=NS, a=2),
                                       in0=ps[7][:, 0:32].rearrange("p (s a m) -> p s a m", s=NS, a=2),
                                       in1=sbias[:, :, 4 * g:4 * g + 4].unsqueeze(1).broadcast_to([128, NS, 2, 4]), op=ALU.add)
            vec(fsb, reads=["ps7", P], writes=["smallf"])
            act(lambda e: e.activation(out=pTs[:], in_=smallf[:, 0:32], func=AF.Exp, bias=negM[:, j:j + 1], scale=1.0),
                reads=["smallf", "negM"], writes=["pTs"])

        def emit_samples_b():
            def fso(e):
                ins = None
                for (off, use_v) in ((64, True), (96, False)):
                    for js in range(NS):
                        for par in range(2):
                            if use_v:
                                o2 = (64 if par == 0 else 0) + 128 * g
                                lt = vs[:, js, o2:o2 + 128]
                            else:
                                lt = onespad[:, 64:192] if par == 0 else onespad[:, 0:128]
                            o = (js * 2 + par) * 4
                            ins = e.matmul(ps[7][:, off + js * 4:off + js * 4 + 4], lhsT=lt, rhs=pTs[:, o:o + 4],
                                           start=(par == 0), stop=(par == 1))
                return ins
            pe(fso, reads=["pTs", "vs", "consts"], writes=["ps7"])
            tv = smallf[:, 32:48].rearrange("p (s m) -> p s m", s=NS)
            chain("dve", [
                lambda e: e.tensor_tensor(out=tv, in0=ps[7][:, 96:112].rearrange("p (s m) -> p s m", s=NS),
                                          in1=esink[:, j, 4 * g:4 * g + 4].unsqueeze(1).broadcast_to([128, NS, 4]), op=ALU.add),
                lambda e: e.reciprocal(out=smallf[:, 32:48], in_=smallf[:, 32:48]),
                lambda e: e.tensor_tensor(out=smallf[:, 32:48], in0=ps[7][:, 64:80], in1=smallf[:, 32:48], op=ALU.mult),
                lambda e: e.tensor_tensor(out=y[:, 4 * g:4 * g + 4, 0:NS], in0=smallf[:, 32:48].rearrange("p (s m) -> p m s", s=NS),
                                          in1=szg[:, :, 0:NS], op=ALU.mult),
            ], reads=["ps7", "esink", szr], writes=["smallf"] + [yres(c, TS) for c in range(4 * g, 4 * g + 4)])
        return emit_S, emit_PV, emit_samples

    def b_layer(p, j):
        tiles = [TS, T1] if p == 0 else [T1]
        rmsnorm(3 + j, tiles, mode=("reuse" if j == 0 else "stats_ready"))
        qa, qb_ = b_q_steps(p, j, 0, tiles)
        for m in range(4):
            qa[m]()
            if m > 0:
                qb_[m - 1]()
        qb_[3]()
        for z_ in b_z_steps(p, j, 0, tiles):
            z_()
        for g in range(4):
            S_, PV_, SM_ = b_att_steps(p, j, g)
            nop = [lambda: None] * 4
            qa, qb_ = b_q_steps(p, j, g + 1, tiles) if g < 3 else (nop, nop)
            zs = b_z_steps(p, j, g + 1, tiles) if g < 3 else []
            S_(1)
            qa[0]()
            S_(2)
            qa[1]()
            qb_[0]()
            PV_(1)
            qa[2]()
            qb_[1]()
            S_(3)
            qa[3]()
            qb_[2]()
            PV_(2)
            qb_[3]()
            S_(4)
            PV_(3)
            zl = list(zs)
            if zl:
                zl[0]()
            PV_(4)
            if p == 0:
                SM_(0)
            if len(zl) > 1:
                zl[1]()
            if p == 0:
                SM_(1)
            for z_ in zl[2:]:
                z_()
        w_out_phase(w_out_b[j], tiles, norm_tiles=(tiles if j == 0 else None))

    outs = []

    def store_y(p):
        jobs = [(T1[0] + i * 128, 128, y_main[p * MP + i * 128:p * MP + (i + 1) * 128, :]) for i in range(4)]
        if p == 0:
            jobs.append((0, NS, y_samp))
        for ji, (c0, n, dst) in enumerate(jobs):
            st, sr = stgs[ji % 2], STGR[ji % 2]
            for cq in range(4):
                bk = 6 + (cq % 2)

                def fp(e, cq=cq, bk=bk, c0=c0, n=n):
                    ins = None
                    for i in range(4):
                        ins = e.transpose(ps[bk][0:n, i * 128:(i + 1) * 128], xres[:, 4 * cq + i, c0:c0 + n], ident[:])
                    return ins
                pe(fp, reads=xall(ALLT) + [P], writes=[f"ps{bk}"])
                vec(lambda e, cq=cq, bk=bk, n=n, st=st: e.tensor_copy(out=st[0:n, cq * 512:(cq + 1) * 512], in_=ps[bk][0:n, :]),
                    reads=[f"ps{bk}"], writes=[sr] + (YALL if (ji == 0 and cq == 0) else []))
            outs.append(dma_sp(dst, st[0:n, :], reads=[sr], dkey=sr))

    def store_states(p):
        for l in range(2):
            st, sr = stgs[l % 2], STGR[l % 2]
            for cq in range(4):
                bk = 6 + (cq % 2)
                if p == 0:
                    def fp(e, cq=cq, bk=bk, l=l):
                        ins = None
                        for i in range(4):
                            ins = e.transpose(ps[bk][0:NS, i * 128:(i + 1) * 128], unew[:, l, 4 * cq + i, :], ident[:])
                        return ins
                    pe(fp, reads=["unew", P], writes=[f"ps{bk}"])
                    n = NS
                else:
                    def fp(e, cq=cq, bk=bk, l=l):
                        ins = None
                        for i in range(4):
                            ins = e.transpose(ps[bk][0:16, i * 128:(i + 1) * 128], ustate[:, l, 4 * cq + i, :], ident[:])
                        return ins
                    pe(fp, reads=["ustate", P], writes=[f"ps{bk}"])
                    n = 16
                vec(lambda e, cq=cq, bk=bk, n=n, st=st: e.tensor_copy(out=st[0:n, cq * 512:(cq + 1) * 512], in_=ps[bk][0:n, :]),
                    reads=[f"ps{bk}"], writes=[sr])
            if p == 0:
                outs.append(dma_sp(pool_s[l, :, 14, :], st[0:NS, :], reads=[sr], dkey=sr))
            else:
                outs.append(dma_sp(pool_p[l], st[1:16, :], reads=[sr], dkey=sr))

    for p in range(NPASS):
        if p == 0:
            load_caches()
        load_x(p)
        if p == 0:
            load_spool(0)
        a_layer(p, 0)
        a_layer(p, 1)
        kv_phase(p)
        b_layer(p, 0)
        b_layer(p, 1)
        store_y(p)
        store_states(p)

    eng_names = ["pe", "dve", "act", "pool", "sp"]
    cnt = {e: 0 for e in eng_names}
    for op in S.ops:
        if op.signal:
            cnt[op.eng] += 1
            op.sigval = cnt[op.eng]
    final_waits = {}
    for key, v in S.dcnt.items():
        final_waits[key] = v
    esem = {e: es.enter_context(nc.semaphore(f"sem_{e}")) for e in eng_names}
    dsem = {k: es.enter_context(nc.semaphore(f"dsem_{k}")) for k in S.dcnt}
    block = es.enter_context(nc.Block())

    def emit(engname):
        def body(e):
            waited = {}
            for op in S.ops:
                if op.eng != engname:
                    continue
                for d in op.deps:
                    if d.dkey is not None:
                        sem, val, key = dsem[d.dkey], d.dval, ("d", d.dkey)
                    else:
                        sem, val, key = esem[d.eng], d.sigval, ("e", d.eng)
                    if waited.get(key, 0) >= val:
                        continue
                    e.wait_ge(sem, val)
                    waited[key] = val
                ins = op.fn(e)
                if op.dkey is not None:
                    ins.then_inc(dsem[op.dkey], 16)
                elif op.signal:
                    ins.then_inc(esem[engname], 1)
            if engname == "sp":
                for key, v in final_waits.items():
                    e.wait_ge(dsem[key], v)
        return body

    block.tensor(emit("pe"))
    block.vector(emit("dve"))
    block.scalar(emit("act"))
    block.gpsimd(emit("pool"))
    block.sync(emit("sp"))
    es.close()
    return nc


_CACHE = {}


def _fm(v):
    return np.ascontiguousarray(np.asarray(v, np.float32).reshape(NCH, 128).T)


def kernel(x_prompt, x_sample, state_pool, cache_k_win, cache_v_win,
           norm_a, w_in_a, w_grp_a, scale_a, w_out_a,
           norm_kv, w_kv, k_norm,
           norm_b, w_in_b, q_norm, sinks, w_out_b):
    f = lambda a: np.ascontiguousarray(np.asarray(a, dtype=np.float32))
    x_prompt, x_sample, state_pool = f(x_prompt), f(x_sample), f(state_pool)
    cache_k_win, cache_v_win = f(cache_k_win), f(cache_v_win)
    w_in_a, w_grp_a, w_out_a, w_kv, w_in_b, w_out_b = map(f, (w_in_a, w_grp_a, w_out_a, w_kv, w_in_b, w_out_b))
    norm_a, scale_a, norm_kv, k_norm, norm_b, q_norm, sinks = map(f, (norm_a, scale_a, norm_kv, k_norm, norm_b, q_norm, sinks))

    if "nc" not in _CACHE:
        _CACHE["nc"] = build_program()
    nc = _CACHE["nc"]

    gains = np.stack([_fm(norm_a[0]), _fm(norm_a[1]), _fm(norm_kv), _fm(norm_b[0]), _fm(norm_b[1]),
                      _fm(scale_a[0]), _fm(scale_a[1])], axis=1).reshape(128, 7 * NCH)
    sinks_l = np.zeros((128, 2, NCH), np.float32)
    for j in range(2):
        sinks_l[0:64, j, :] = sinks[j, 0::2][None, :]
        sinks_l[64:128, j, :] = sinks[j, 1::2][None, :]
    sinks_l = sinks_l.reshape(128, 2 * NCH)
    gk_dup = np.concatenate([k_norm, k_norm]).reshape(128, 1)
    gq_dup = np.stack([np.concatenate([q_norm[0], q_norm[0]]), np.concatenate([q_norm[1], q_norm[1]])], axis=1)
    qk_rows = np.stack([q_norm[0], q_norm[1], k_norm], axis=0)
    ident = np.eye(128, dtype=np.float32)
    kk = np.arange(128)[:, None]
    qq = np.arange(128)[None, :]
    dcur = np.where(qq >= kk, (qq - kk).astype(np.float32), BIG).astype(np.float32)
    dprev = np.where(kk > qq, (qq - kk + 128).astype(np.float32), BIG).astype(np.float32)
    dbig = np.full((128, 128), BIG, np.float32)
    slopes = 2.0 ** (-(np.arange(32) + 1) / 4.0)
    sb_tab = np.zeros((128, 2, NCH), np.float32)
    for par in range(2):
        for c in range(NCH):
            sb_tab[:, par, c] = -slopes[2 * c + par] * (127 - np.arange(128))
    sb_tab = sb_tab.reshape(128, 2 * NCH)

    in_maps = []
    for core in range(NCORES):
        b, s = core // 4, core % 4
        t0 = s * MP * NPASS
        xcore = np.zeros((NROWS, D), np.float32)
        xcore[0:NS] = x_sample[core * NS:(core + 1) * NS, 0, :]
        if s > 0:
            xcore[NS:NS + HALO] = x_prompt[b, t0 - HALO:t0, :]
        xcore[NS + HALO:] = x_prompt[b, t0:t0 + MP * NPASS, :]
        invc = np.zeros((128, 4, 16), np.float32)
        for g in range(4):
            w = 2 << g
            if s == 0:
                invc[:, g, :] = (1.0 / np.minimum(np.arange(16) + 1, w))[None, :]
            else:
                invc[:, g, :] = 1.0 / w
        dist = np.stack([dcur, dprev, dbig if s == 0 else dprev], axis=1).reshape(128, 3 * 128)
        in_maps.append({
            "xc": xcore,
            "spool": np.ascontiguousarray(state_pool[:, core * NS:(core + 1) * NS].reshape(2, NS * 15, D)),
            "ck": np.ascontiguousarray(cache_k_win[core * NS:(core + 1) * NS].reshape(NS, 128, 256)),
            "cv": np.ascontiguousarray(cache_v_win[core * NS:(core + 1) * NS].reshape(NS, 128, 256)),
            "w_in_a": w_in_a, "w_grp_a": w_grp_a, "w_out_a": w_out_a, "w_kv": w_kv,
            "w_in_b": w_in_b, "w_out_b": w_out_b,
            "gains": np.ascontiguousarray(gains), "sinks_l": sinks_l, "gk_dup": np.ascontiguousarray(gk_dup),
            "gq_dup": np.ascontiguousarray(gq_dup), "qk_rows": np.ascontiguousarray(qk_rows),
            "ident": ident, "dist": np.ascontiguousarray(dist), "sbias": sb_tab,
            "invc": np.ascontiguousarray(invc.reshape(128, 64)),
        })

    res = run_bass_kernel_spmd(nc, in_maps, core_ids=list(range(NCORES)))
    R = res.results
    B, SEQ = x_prompt.shape[0], x_prompt.shape[1]
    y_prompt = np.zeros((B, SEQ, D), np.float32)
    y_sample = np.zeros((NCORES * NS, 1, D), np.float32)
    pool_p = np.zeros((2, B, 15, D), np.float32)
    pool_s = np.zeros((2, NCORES * NS, 15, D), np.float32)
    kwp = np.zeros((B, 128, 4, 64), np.float32)
    vwp = np.zeros((B, 128, 4, 64), np.float32)
    kws = np.zeros((NCORES * NS, 128, 4, 64), np.float32)
    vws = np.zeros((NCORES * NS, 128, 4, 64), np.float32)
    for core in range(NCORES):
        b, s = core // 4, core % 4
        r = R[core]
        y_prompt[b, s * 1024:(s + 1) * 1024] = r["y_main"]
        y_sample[core * NS:(core + 1) * NS, 0] = r["y_samp"]
        pool_s[:, core * NS:(core + 1) * NS] = r["pool_s"]
        kws[core * NS:(core + 1) * NS] = r["kwin_s"].reshape(NS, 128, 4, 64)
        vws[core * NS:(core + 1) * NS] = r["vwin_s"].reshape(NS, 128, 4, 64)
        if s == 3:
            pool_p[:, b] = r["pool_p"]
            kwp[b] = r["kwin_p"].reshape(128, 4, 64)
            vwp[b] = r["vwin_p"].reshape(128, 4, 64)
    return (y_prompt, y_sample, pool_p, pool_s, kwp, vwp, kws, vws)
```
